# Optimizing a Trainium2 kernel written in Bass

```python
import jax, jax.numpy as jnp
from jax import lax
import numpy as np

D_MODEL = 1024
BATCH = 8
SEQ = 2048
DEPTH = 2
DEC_BATCH = 8
DEC_SEQ = 16
PAST_LEN = 2048

CHUNK = 64
N_EVEN = (DEPTH + 1) // 2
N_ODD = DEPTH // 2
D_A = D_MODEL // 2
SCONV_W = 3
H_B = 8
HD_B = D_MODEL // 2 // H_B
D_B = H_B * HD_B
Q_BLOCK = 128
D_C = D_MODEL // 2
C_GROUPS = 8
C_GW = D_C // C_GROUPS
C_CHUNK = 128
D_D = D_MODEL // 2
CCONV_W = 31
D_FF = 2816
FORGET_BIAS = 3.0
EPS = 1e-6
E_SPLITS = (D_A, 2 * D_A, 3 * D_A, 3 * D_A + D_B, 3 * D_A + 2 * D_B, 3 * D_A + 3 * D_B)
E_COLS = 3 * D_A + 3 * D_B + H_B
O_SPLITS = (D_C, 2 * D_C, 2 * D_C + D_D)
O_COLS = 2 * D_C + 2 * D_D

kernel_name = 'hybrid_streaming_encoder_step'


def rmsnorm(x, g):
    xf = x.astype(jnp.float32)
    y = xf * lax.rsqrt(jnp.mean(xf * xf, axis=-1, keepdims=True) + EPS)
    return y.astype(x.dtype) * g


def layernorm(x, g, b):
    xf = x.astype(jnp.float32)
    mu = jnp.mean(xf, axis=-1, keepdims=True)
    xc = xf - mu
    y = xc * lax.rsqrt(jnp.mean(xc * xc, axis=-1, keepdims=True) + EPS)
    return y.astype(x.dtype) * g + b


def swiglu(h, wg, wu, wd):
    return (jax.nn.silu(h @ wg) * (h @ wu)) @ wd


def causal_dwconv(x, ctx, w):
    xp = jnp.concatenate([ctx.astype(x.dtype), x], axis=1)
    y = lax.conv_general_dilated(xp, w[:, None, :].astype(x.dtype), (1,), 'VALID',
                                 dimension_numbers=('NWC', 'WIO', 'NWC'),
                                 feature_group_count=x.shape[-1])
    return y, xp[:, xp.shape[1] - (w.shape[0] - 1):]


def fox_block(q, c_q, q_pos, k, v, c_k):
    s = jnp.einsum('bqhd,bkhd->bhqk', q, k, preferred_element_type=jnp.float32) * (HD_B ** -0.5)
    s = s + jnp.transpose(c_q, (0, 2, 1))[:, :, :, None] - jnp.transpose(c_k, (0, 2, 1))[:, :, None, :]
    mask = jnp.arange(k.shape[1])[None, :] <= q_pos[:, None]
    s = jnp.where(mask[None, None], s, -jnp.inf)
    p = jax.nn.softmax(s, axis=-1).astype(v.dtype)
    return jnp.einsum('bhqk,bkhd->bqhd', p, v)


def even_mixer(h, w_in, b_f, conv_w, w_out, sconv_ctx, past):
    bsz, t, _ = h.shape
    z = h @ w_in
    xa, gb, gc, q, k, v, fl = jnp.split(z, E_SPLITS, axis=-1)
    ca, new_sconv = causal_dwconv(gc * xa, sconv_ctx, conv_w)
    ya = gb * ca
    q = q.reshape(bsz, t, H_B, HD_B)
    k = k.reshape(bsz, t, H_B, HD_B)
    v = v.reshape(bsz, t, H_B, HD_B)
    logf = jax.nn.log_sigmoid(fl.astype(jnp.float32) + b_f.astype(jnp.float32))
    if past is None:
        c = jnp.cumsum(logf, axis=1)
        nb = t // Q_BLOCK
        qb = jnp.transpose(q.reshape(bsz, nb, Q_BLOCK, H_B, HD_B), (1, 0, 2, 3, 4))
        cb = jnp.transpose(c.reshape(bsz, nb, Q_BLOCK, H_B), (1, 0, 2, 3))
        pos = jnp.arange(t).reshape(nb, Q_BLOCK)
        yb = lax.map(lambda a: fox_block(a[0], a[1], a[2], k, v, c), (qb, cb, pos))
        yb = jnp.transpose(yb, (1, 0, 2, 3, 4)).reshape(bsz, t, D_B)
    else:
        pk, pv, plf = past
        p_len = pk.shape[1]
        k_all = jnp.concatenate([pk.astype(k.dtype), k], axis=1)
        v_all = jnp.concatenate([pv.astype(v.dtype), v], axis=1)
        c_all = jnp.cumsum(jnp.concatenate([plf.astype(jnp.float32), logf], axis=1), axis=1)
        yb = fox_block(q, c_all[:, p_len:], p_len + jnp.arange(t), k_all, v_all, c_all)
        yb = yb.reshape(bsz, t, D_B)
    y = jnp.concatenate([ya, yb], axis=-1) @ w_out
    return y, k, v, logf, new_sconv


def spatial_gate(v, ws, bs):
    bsz, t, _ = v.shape
    L = min(t, C_CHUNK)
    n = t // L
    vr = v.reshape(bsz, n, L, C_GROUPS, C_GW)
    w = jnp.where(jnp.tril(jnp.ones((L, L), dtype=bool))[None], ws[:, :L, :L], 0).astype(v.dtype)
    s = jnp.einsum('gts,bnsgc->bntgc', w, vr) + jnp.transpose(bs[:, :L])[None, None, :, :, None].astype(v.dtype)
    return s.reshape(bsz, t, D_C)


def odd_mixer(h, w_in, ln_g, ln_b, ws, bs, dw, dw_b, cn_g, cn_b, w_out, cconv_ctx):
    z = h @ w_in
    u, vv, ga, gb = jnp.split(z, O_SPLITS, axis=-1)
    u = jax.nn.gelu(u)
    vv = layernorm(jax.nn.gelu(vv), ln_g, ln_b)
    yc = u * spatial_gate(vv, ws, bs)
    g = ga * jax.nn.sigmoid(gb)
    cd, new_cconv = causal_dwconv(g, cconv_ctx, dw)
    yd = jax.nn.silu(layernorm(cd + dw_b, cn_g, cn_b))
    y = jnp.concatenate([yc, yd], axis=-1) @ w_out
    return y, vv, new_cconv


def run_trunk(x, prm, sconv_ctx, cconv_ctx, past):
    ks, vs, lfs, scs, ccs, gvs = [], [], [], [], [], []
    for l in range(DEPTH):
        i = l // 2
        x = x + 0.5 * swiglu(rmsnorm(x, prm['ffn1_g'][l]), prm['ffn1_wg'][l], prm['ffn1_wu'][l], prm['ffn1_wd'][l])
        h = rmsnorm(x, prm['mix_g'][l])
        if l % 2 == 0:
            layer_past = None if past is None else (past[0][i], past[1][i], past[2][i])
            y, k, v, lf, sc = even_mixer(h, prm['e_w_in'][i], prm['e_b_f'][i], prm['e_conv_w'][i],
                                         prm['e_w_out'][i], sconv_ctx[i], layer_past)
            ks.append(k); vs.append(v); lfs.append(lf); scs.append(sc)
        else:
            y, gv, cc = odd_mixer(h, prm['o_w_in'][i], prm['o_ln_g'][i], prm['o_ln_b'][i], prm['o_ws'][i],
                                  prm['o_bs'][i], prm['o_dw'][i], prm['o_dw_b'][i], prm['o_cn_g'][i],
                                  prm['o_cn_b'][i], prm['o_w_out'][i], cconv_ctx[i])
            gvs.append(gv); ccs.append(cc)
        x = x + y
        x = x + 0.5 * swiglu(rmsnorm(x, prm['ffn2_g'][l]), prm['ffn2_wg'][l], prm['ffn2_wu'][l], prm['ffn2_wd'][l])
    return (rmsnorm(x, prm['final_g']), jnp.stack(ks), jnp.stack(vs), jnp.stack(lfs),
            jnp.stack(scs), jnp.stack(ccs), jnp.stack(gvs))


def setup_inputs(seed: int = 0) -> dict:
    key = jax.random.key(seed)
    ks = jax.random.split(key, 40)
    f32 = jnp.float32

    def nrm(k, shape, scale):
        return jax.random.normal(k, shape, f32) * scale

    D = D_MODEL
    return {
        'x_prompt': nrm(ks[0], (BATCH, SEQ, D), 1.0),
        'x_sample': nrm(ks[1], (DEC_BATCH, DEC_SEQ, D), 1.0),
        'cache_fox_k': nrm(ks[2], (N_EVEN, DEC_BATCH, PAST_LEN, H_B, HD_B), 1.0),
        'cache_fox_v': nrm(ks[3], (N_EVEN, DEC_BATCH, PAST_LEN, H_B, HD_B), 1.0),
        'cache_fox_logf': jax.nn.log_sigmoid(FORGET_BIAS + nrm(ks[4], (N_EVEN, DEC_BATCH, PAST_LEN, H_B), 1.0)),
        'state_sconv': nrm(ks[5], (N_EVEN, DEC_BATCH, SCONV_W - 1, D_A), 1.0),
        'state_cconv': nrm(ks[6], (N_ODD, DEC_BATCH, CCONV_W - 1, D_D), 0.5),
        'ffn1_g': 1.0 + nrm(ks[7], (DEPTH, D), 0.02),
        'ffn1_wg': nrm(ks[8], (DEPTH, D, D_FF), D ** -0.5),
        'ffn1_wu': nrm(ks[9], (DEPTH, D, D_FF), D ** -0.5),
        'ffn1_wd': nrm(ks[10], (DEPTH, D_FF, D), D_FF ** -0.5),
        'mix_g': 1.0 + nrm(ks[11], (DEPTH, D), 0.02),
        'ffn2_g': 1.0 + nrm(ks[12], (DEPTH, D), 0.02),
        'ffn2_wg': nrm(ks[13], (DEPTH, D, D_FF), D ** -0.5),
        'ffn2_wu': nrm(ks[14], (DEPTH, D, D_FF), D ** -0.5),
        'ffn2_wd': nrm(ks[15], (DEPTH, D_FF, D), D_FF ** -0.5),
        'e_w_in': nrm(ks[16], (N_EVEN, D, E_COLS), D ** -0.5),
        'e_b_f': FORGET_BIAS + nrm(ks[17], (N_EVEN, H_B), 0.1),
        'e_conv_w': nrm(ks[18], (N_EVEN, SCONV_W, D_A), SCONV_W ** -0.5),
        'e_w_out': nrm(ks[19], (N_EVEN, D_A + D_B, D), (D_A + D_B) ** -0.5),
        'o_w_in': nrm(ks[20], (N_ODD, D, O_COLS), D ** -0.5),
        'o_ln_g': 1.0 + nrm(ks[21], (N_ODD, D_C), 0.02),
        'o_ln_b': nrm(ks[22], (N_ODD, D_C), 0.02),
        'o_ws': nrm(ks[23], (N_ODD, C_GROUPS, C_CHUNK, C_CHUNK), 0.5 * C_CHUNK ** -0.5),
        'o_bs': 1.0 + nrm(ks[24], (N_ODD, C_GROUPS, C_CHUNK), 0.02),
        'o_dw': nrm(ks[25], (N_ODD, CCONV_W, D_D), CCONV_W ** -0.5),
        'o_dw_b': nrm(ks[26], (N_ODD, D_D), 0.02),
        'o_cn_g': 1.0 + nrm(ks[27], (N_ODD, D_D), 0.02),
        'o_cn_b': nrm(ks[28], (N_ODD, D_D), 0.02),
        'o_w_out': nrm(ks[29], (N_ODD, D_C + D_D, D), (D_C + D_D) ** -0.5),
        'final_g': 1.0 + nrm(ks[30], (D,), 0.02),
    }


def reference(x_prompt, x_sample, cache_fox_k, cache_fox_v, cache_fox_logf, state_sconv, state_cconv,
              ffn1_g, ffn1_wg, ffn1_wu, ffn1_wd, mix_g, ffn2_g, ffn2_wg, ffn2_wu, ffn2_wd,
              e_w_in, e_b_f, e_conv_w, e_w_out, o_w_in, o_ln_g, o_ln_b, o_ws, o_bs,
              o_dw, o_dw_b, o_cn_g, o_cn_b, o_w_out, final_g):
    prm = dict(ffn1_g=ffn1_g, ffn1_wg=ffn1_wg, ffn1_wu=ffn1_wu, ffn1_wd=ffn1_wd, mix_g=mix_g,
               ffn2_g=ffn2_g, ffn2_wg=ffn2_wg, ffn2_wu=ffn2_wu, ffn2_wd=ffn2_wd,
               e_w_in=e_w_in, e_b_f=e_b_f, e_conv_w=e_conv_w, e_w_out=e_w_out,
               o_w_in=o_w_in, o_ln_g=o_ln_g, o_ln_b=o_ln_b, o_ws=o_ws, o_bs=o_bs,
               o_dw=o_dw, o_dw_b=o_dw_b, o_cn_g=o_cn_g, o_cn_b=o_cn_b, o_w_out=o_w_out,
               final_g=final_g)
    bp = x_prompt.shape[0]
    zero_sconv = jnp.zeros((N_EVEN, bp, SCONV_W - 1, D_A), x_prompt.dtype)
    zero_cconv = jnp.zeros((N_ODD, bp, CCONV_W - 1, D_D), x_prompt.dtype)
    y_prompt, p_fox_k, p_fox_v, p_fox_logf, p_sconv, p_cconv, _p_gv = run_trunk(
        x_prompt, prm, zero_sconv, zero_cconv, None)
    y_sample, s_fox_k, s_fox_v, s_fox_logf, s_sconv, s_cconv, s_gmlp_v = run_trunk(
        x_sample, prm, state_sconv, state_cconv, (cache_fox_k, cache_fox_v, cache_fox_logf))
    return (y_prompt, y_sample, p_fox_k, p_fox_v, p_fox_logf, p_sconv, p_cconv,
            s_fox_k, s_fox_v, s_fox_logf, s_sconv, s_cconv, s_gmlp_v)
```

```python
import os
from contextlib import ExitStack
import numpy as np
import concourse.bass as bass
import concourse.mybir as mybir
from concourse.bass_utils import run_bass_kernel_spmd

F32 = mybir.dt.float32
BF16 = mybir.dt.bfloat16
AF = mybir.ActivationFunctionType
ALU = mybir.AluOpType

NCORES = 8
D = 1024
DC = 8
SEQ = 2048
NS = 16
NT = SEQ + NS
DFF = 2816
NF = 22
EPS = 1e-6
TT = [(0, 512), (512, 512), (1024, 512), (1536, 512), (2048, 16)]
TK = [(j * 128, 128) for j in range(16)] + [(2048, 16)]
NEG = -30000.0


class Res:
    __slots__ = ("name", "last_w", "readers", "excl", "key")

    def __init__(self, name, excl=False):
        self.name = name
        self.last_w = None
        self.readers = []
        self.excl = excl
        self.key = None


class DmaKey:
    __slots__ = ("name", "count", "sem", "last_op", "is_w")

    def __init__(self, name, is_w=False):
        self.name = name
        self.count = 0
        self.sem = None
        self.last_op = None
        self.is_w = is_w


class Op:
    __slots__ = ("eng", "fn", "pos", "is_dma", "key", "kcount", "waits", "need_inc", "clock", "inc_count")

    def __init__(self, eng, fn, is_dma, key):
        self.eng = eng
        self.fn = fn
        self.is_dma = is_dma
        self.key = key
        self.kcount = 0
        self.pos = -1
        self.waits = {}
        self.need_inc = False
        self.clock = None
        self.inc_count = 0

    def signal(self):
        if self.is_dma:
            return (self.key, self.kcount)
        return (self.eng, self.pos + 1)


ENGS = ("pe", "act", "dve", "pool", "sp")


class Prog:
    def __init__(self):
        self.streams = {e: [] for e in ENGS}
        self.seen = {e: {} for e in ENGS}
        self.pending = {e: [] for e in ENGS}
        self.keys = []
        self.same_sync = {"act", "dve", "pool", "sp"}
        self.self_waited = {e: 0 for e in ENGS}

    def res(self, name, excl=False):
        return Res(name, excl)

    def kof(self, r):
        if r.key is None:
            r.key = self.key("r" + str(len(self.keys)) + "_" + r.name)
        return r.key

    def key(self, name, is_w=False):
        k = DmaKey(name, is_w)
        self.keys.append(k)
        return k

    def _add(self, eng, fn, reads, writes, is_dma=False, key=None):
        if isinstance(key, Res):
            key = self.kof(key)
        op = Op(eng, fn, is_dma, key)
        stream = self.streams[eng]
        op.pos = len(stream)
        ex = [r for r in reads if r.excl]
        if ex:
            reads = [r for r in reads if not r.excl]
            writes = list(writes) + ex
        deps = []
        for r in reads:
            if r.last_w is not None:
                deps.append(r.last_w)
        for w in writes:
            if w.last_w is not None:
                deps.append(w.last_w)
            deps.extend(w.readers)
        if self.pending[eng]:
            deps.extend(self.pending[eng])
            self.pending[eng] = []
        seen = self.seen[eng]
        for d in deps:
            if d is op:
                continue
            if (not d.is_dma) and d.eng == eng:
                if eng not in self.same_sync:
                    continue
                if self.self_waited[eng] >= d.pos + 1:
                    continue
                self.self_waited[eng] = d.pos + 1
                if op.waits.get(eng, 0) < d.pos + 1:
                    op.waits[eng] = d.pos + 1
                d.need_inc = True
                continue
            name, val = d.signal()
            if d.is_dma:
                val = d.key.count
            if seen.get(name, 0) >= val:
                continue
            if op.waits.get(name, 0) < val:
                op.waits[name] = val
            seen[name] = val
            if not d.is_dma:
                d.need_inc = True
            for n2, v2 in d.clock.items():
                if seen.get(n2, 0) < v2:
                    seen[n2] = v2
        if is_dma:
            key.count += 1
            op.kcount = key.count
            key.last_op = op
            clock = dict(seen)
            clock[key] = op.kcount
        else:
            seen[eng] = op.pos + 1
            clock = dict(seen)
        op.clock = clock
        stream.append(op)
        for r in reads:
            r.readers.append(op)
        for w in writes:
            w.last_w = op
            w.readers = []
        return op

    def op(self, eng, fn, reads=(), writes=()):
        return self._add(eng, fn, reads, writes)

    def dma(self, eng, fn, key, reads=(), writes=()):
        return self._add(eng, fn, reads, writes, is_dma=True, key=key)

    def barrier(self):
        lasts = []
        for e in ("pe", "act", "dve", "sp"):
            for o in reversed(self.streams[e]):
                if not o.is_dma:
                    lasts.append(o)
                    break
        for k in self.keys:
            if k.last_op is not None and not k.is_w:
                lasts.append(k.last_op)
        for e in ("act", "dve", "sp"):
            self.pending[e] = self.pending[e] + list(lasts)

    def emit(self, nc, stack):
        esem = {}
        for e in ENGS:
            esem[e] = stack.enter_context(nc.semaphore("s_" + e))
        for k in self.keys:
            k.sem = stack.enter_context(nc.semaphore("k_" + k.name))
        for e in ENGS:
            c = 0
            for o in self.streams[e]:
                if (not o.is_dma) and o.need_inc:
                    c += 1
                o.inc_count = c
        streams = self.streams
        final = [(k.sem, 16 * k.count) for k in self.keys if k.count]

        def run(e, eng):
            for o in streams[e]:
                for name, val in o.waits.items():
                    if isinstance(name, DmaKey):
                        eng.wait_ge(name.sem, 16 * val)
                    else:
                        eng.wait_ge(esem[name], streams[name][val - 1].inc_count)
                ins = o.fn(eng)
                if o.is_dma:
                    ins.then_inc(o.key.sem, 16)
                elif o.need_inc:
                    ins.then_inc(esem[e], 1)
            if e == "sp":
                for s, v in final:
                    eng.wait_ge(s, v)

        block = stack.enter_context(nc.Block())

        @block.tensor
        def _(eng):
            run("pe", eng)

        @block.scalar
        def _(eng):
            run("act", eng)

        @block.vector
        def _(eng):
            run("dve", eng)

        @block.gpsimd
        def _(eng):
            run("pool", eng)

        @block.sync
        def _(eng):
            run("sp", eng)


class WItem:
    __slots__ = ("pool", "loads", "slot", "issued", "done", "ap", "res")

    def __init__(self, pool, loads):
        self.pool = pool
        self.loads = loads
        self.slot = -1
        self.issued = False
        self.done = False


class WQ:
    def __init__(self, p, pools):
        self.p = p
        self.pools = pools
        self.items = []
        self.next_issue = 0
        self.next_use = 0
        self.occ = {n: [None] * len(v) for n, v in pools.items()}
        self.cnt = {n: 0 for n in pools}

    def add(self, pool, loads):
        self.items.append(WItem(pool, loads))

    def pump(self):
        while self.next_issue < len(self.items):
            it = self.items[self.next_issue]
            ring = self.pools[it.pool]
            s = self.cnt[it.pool] % len(ring)
            prev = self.occ[it.pool][s]
            if prev is not None and not prev.done:
                return
            ap, res, key = ring[s]
            it.slot, it.ap, it.res = s, ap, res
            for ld in it.loads:
                self.p.dma("pool", (lambda e, ld=ld, ap=ap: ld(e, ap)), key, writes=[res])
            it.issued = True
            self.occ[it.pool][s] = it
            self.cnt[it.pool] += 1
            self.next_issue += 1

    def next(self):
        self.pump()
        it = self.items[self.next_use]
        assert it.issued, "weight item not issuable (ring too small for outstanding items)"
        self.next_use += 1
        return it

    def release(self, it):
        it.done = True
        self.pump()


E_ORDER = [0, 8, 4, 1, 9, 5, 2, 10, 6, 3, 11, 7, 12, 16, 13, 17, 14, 18, 15, 19]
O_ORDER = [0, 1, 2, 3, 8, 12, 9, 13, 10, 14, 11, 15]
PV_G = 0
PV_CW = 56
PV_DW = 68
PV_DWB = 192
PV_CNG = 196
PV_CNB = 200
NPV = 204
PB_LNG = 0
PB_LNB = 512
PB_BF = 1024
NPB = 1024 + 17 * 8


def build_nc(stages=None):
    if stages is None:
        stages = os.environ.get("MK_STAGES", "f0,em,f1,f2,om,f3,fn").split(",")
    nc = bass.Bass("TRN2", target_bir_lowering=False)

    def din(name, shape):
        return nc.dram_tensor(name, list(shape), F32, kind="ExternalInput").ap()

    def dout(name, shape):
        return nc.dram_tensor(name, list(shape), F32, kind="ExternalOutput").ap()

    x_in = din("xT_in", [D, NT])
    wgu_in = din("wgu", [4, NF, 128, 2048])
    wd_in = din("wd", [4, DFF, D])
    ewa_in = din("ewa", [10, 128, 2048])
    ewv_in = din("ewv", [128, 8, 520])
    ewo_in = din("ewo", [D, D])
    owa_in = din("owa", [6, 128, 2048])
    owv_in = din("owv", [128, 8, 512])
    owo_in = din("owo", [D, D])
    pv_in = din("pv", [128, NPV])
    pb_in = din("pb", [128, NPB])
    wst_in = din("wst", [128, 8, 128])
    bs_in = din("bsr", [1, 1024])
    ckt_in = din("ckt", [512, SEQ])
    cv_in = din("cv", [4, 128, 2048])
    clf_in = din("clf", [128, 128])
    ssc_in = din("ssc", [512, 2])
    scc_in = din("scc", [512, 30])

    yT_o = dout("yT", [D, NT])
    kT_o = dout("kT", [512, NT])
    v_o = dout("vv", [NT, 512])
    lf_o = dout("lf", [128, 136])
    scp_o = dout("scp", [512, 2])
    scs_o = dout("scs", [512, 2])
    ccp_o = dout("ccp", [512, 30])
    ccs_o = dout("ccs", [512, 30])
    gv_o = dout("gv", [NS, 512])

    p = Prog()
    MARKS.clear()
    st = ExitStack()
    with st:
        def sb(name, shape, dt):
            return st.enter_context(nc.sbuf_tensor(name, shape, dt))

        xT = sb("xT", [128, DC, NT], F32)
        r_xT = [[p.res(f"xT{c}_{t}") for t in range(5)] for c in range(DC)]
        poolA_t = sb("poolA", [128, 4, 2048], BF16)
        poolB_t = sb("poolB", [128, 12, 1024], BF16)
        pv = sb("pv_sb", [128, NPV], F32)
        pbc = sb("pbc", [128, NPB], F32)
        ident_f = sb("ident_f", [128, 128], F32)
        ident_b = sb("ident_b", [128, 128], BF16)
        ones_b = sb("ones_b", [128, 128], BF16)
        ones_f = sb("ones_f", [128, 128], F32)
        tri_f = sb("tri_f", [128, 128], F32)
        esel = sb("esel", [128, 2, 128], F32)
        maskneg = sb("maskneg", [128, 128], BF16)
        wstm = sb("wstm", [128, 8, 128], BF16)
        ARENA_W = 23700
        arena = sb("arena", [128, ARENA_W], F32)
        r_const = p.res("const")

        PS = [st.enter_context(nc.psum_tensor(f"ps{i}", [128, 512], F32)) for i in range(8)]
        r_ps = [p.res(f"ps{i}", excl=True) for i in range(8)]


        class Carver:
            def __init__(self):
                self.off = 0

            def f32(self, n):
                a = arena[:, self.off:self.off + n]
                self.off += n
                assert self.off <= ARENA_W, self.off
                return a

            def bf16(self, n):
                w = (n + 1) // 2
                a = arena[:, self.off:self.off + w].bitcast(BF16)
                self.off += w
                assert self.off <= ARENA_W, self.off
                return a

        poolA = [(poolA_t[:, s, :], p.res(f"A{s}"), p.key(f"A{s}", True)) for s in range(4)]
        poolB = [(poolB_t[:, s, :], p.res(f"B{s}"), p.key(f"B{s}", True)) for s in range(12)]
        wq = WQ(p, {"A": poolA, "B": poolB})

        def ld_full(src):
            return lambda e, ap: e.dma_start(out=ap, in_=src)

        def ld_part(src, n):
            return lambda e, ap: e.dma_start(out=ap[:, 0:n], in_=src)

        GROUPS = [(0, 4), (4, 4), (8, 4), (12, 4), (16, 6)]

        def add_even_w():
            for s in range(6):
                wq.add("A", [ld_full(ewa_in[s])])
            for fc in range(4):
                wq.add("B", [ld_full(ewo_in[fc * 128:(fc + 1) * 128, :])])
            for c in range(8):
                wq.add("B", [ld_part(ewv_in[:, c, :], 520)])
            for pr in range(4):
                wq.add("A", [ld_full(ewa_in[6 + pr])])
                wq.add("B", [ld_full(ewo_in[(4 + pr) * 128:(5 + pr) * 128, :])])

        def add_odd_w():
            for c in range(8):
                wq.add("B", [ld_part(owv_in[:, c, :], 512)])
            for s in range(2):
                wq.add("A", [ld_full(owa_in[s])])
            for fc in range(4):
                wq.add("B", [ld_full(owo_in[fc * 128:(fc + 1) * 128, :])])
            for s in range(2, 6):
                wq.add("A", [ld_full(owa_in[s])])
            for fc in range(4, 8):
                wq.add("B", [ld_full(owo_in[fc * 128:(fc + 1) * 128, :])])


        r_pv = p.res("pv")
        p.dma("sp", lambda e: e.dma_start(out=pv[:], in_=pv_in), r_pv, writes=[r_pv])
        p.dma("sp", lambda e: e.dma_start(out=pbc[:], in_=pb_in), r_pv, writes=[r_pv])
        wst_f = arena[:, 0:1024]
        r_wstf = p.res("wstf")
        p.dma("sp", lambda e: e.dma_start(out=wst_f.rearrange("p (g t) -> p g t", g=8), in_=wst_in), r_wstf,
              writes=[r_wstf])

        def cset(fn):
            p.op("pool", fn, reads=[r_const], writes=[r_const])

        cset(lambda e: e.memset(ident_f[:], 0.0))
        cset(lambda e: e.affine_select(out=ident_f[:], in_=ident_f[:], compare_op=ALU.not_equal, fill=1.0,
                                       base=0, pattern=[[-1, 128]], channel_multiplier=1))
        cset(lambda e: e.tensor_copy(out=ident_b[:], in_=ident_f[:]))
        cset(lambda e: e.memset(ones_b[:], 1.0))
        cset(lambda e: e.memset(ones_f[:], 1.0))
        cset(lambda e: e.affine_select(out=tri_f[:], in_=ones_f[:], compare_op=ALU.is_ge, fill=0.0,
                                       base=0, pattern=[[1, 128]], channel_multiplier=-1))
        cset(lambda e: e.memset(maskneg[:], 0.0))
        cset(lambda e: e.affine_select(out=maskneg[:], in_=maskneg[:], compare_op=ALU.is_ge, fill=NEG,
                                       base=0, pattern=[[1, 128]], channel_multiplier=-1))
        cset(lambda e: e.memset(esel[:], 0.0))
        cset(lambda e: e.affine_select(out=esel[:, 0, :], in_=esel[:, 0, :], compare_op=ALU.not_equal, fill=1.0,
                                       base=-127, pattern=[[0, 128]], channel_multiplier=1))
        cset(lambda e: e.affine_select(out=esel[:, 1, :], in_=esel[:, 1, :], compare_op=ALU.not_equal, fill=1.0,
                                       base=-15, pattern=[[0, 128]], channel_multiplier=1))
        for g in range(8):
            p.op("dve", lambda e, g=g: e.tensor_tensor(out=wstm[:, g, :], in0=wst_f[:, g * 128:(g + 1) * 128],
                                                       in1=tri_f[:], op=ALU.mult),
                 reads=[r_wstf, r_const], writes=[r_pv])
        p.barrier()
        for t, (t0, n) in enumerate(TT):
            for c in range(DC):
                p.dma("sp", lambda e, c=c, t0=t0, n=n: e.dma_start(out=xT[:, c, t0:t0 + n], in_=x_in[c * 128:(c + 1) * 128, t0:t0 + n]),
                      r_xT[c][t], writes=[r_xT[c][t]])

        def mm(out, lhsT, rhs, start, stop, reads, writes):
            p.op("pe", lambda e: e.matmul(out, lhsT=lhsT, rhs=rhs, start=start, stop=stop, skip_group_check=True),
                 reads=reads, writes=writes)

        XN_W = DC * NT // 2
        xn = arena[:, 0:XN_W].bitcast(BF16).rearrange("p (c n) -> p c n", c=DC)
        r_xn = [[p.res(f"xn{c}_{t}") for t in range(5)] for c in range(DC)]
        rstd2 = [arena[:, XN_W:XN_W + 512]] * 2
        r_rstd2 = [p.res("rstd")] * 2
        BASE = XN_W + 512
        yst = [arena[:, BASE + 6192 + 512 * b_:BASE + 6192 + 512 * (b_ + 1)] for b_ in range(4)]
        r_yst = [p.res(f"yst{b_}") for b_ in range(4)]
        ycnt_f = [0]
        SQ8 = [None]

        def set_sq8(ap2d):
            SQ8[0] = (ap2d.rearrange("p (c n) -> p c n", c=DC), [p.res(f"sq8_{c}") for c in range(DC)])

        def norm_tile(t, gidx, ssb=6, final=False):
            t0, n = TT[t]
            ss = PS[ssb]
            rstd, r_rstd = rstd2[t % 2], r_rstd2[t % 2]
            sq8, r_sq8 = SQ8[0]
            for c in range(DC):
                p.op("act", lambda e, c=c: e.activation(out=sq8[:, c, :n], in_=xT[:, c, t0:t0 + n], func=AF.Square),
                     reads=[r_xT[c][t]], writes=[r_sq8[c]])
            for c in range(DC):
                mm(ss[:, :n], ones_b[:], sq8[:, c, :n], c == 0, c == DC - 1, [r_sq8[c], r_const], [r_ps[ssb]])
            p.op("act", lambda e: e.activation(out=rstd[:, :n], in_=ss[:, :n], func=AF.Ln, scale=1.0 / D, bias=EPS),
                 reads=[r_ps[ssb]], writes=[r_rstd])
            p.op("act", lambda e: e.activation(out=rstd[:, :n], in_=rstd[:, :n], func=AF.Exp, scale=-0.5),
                 reads=[r_rstd], writes=[r_rstd])
            for c in range(DC):
                gs = pv[:, gidx * 8 + c:gidx * 8 + c + 1]
                if not final:
                    p.op("dve", lambda e, c=c, gs=gs: e.scalar_tensor_tensor(
                        out=xn[:, c, t0:t0 + n], in0=xT[:, c, t0:t0 + n], scalar=gs, in1=rstd[:, :n], op0=ALU.mult, op1=ALU.mult),
                        reads=[r_xT[c][t], r_rstd, r_pv], writes=[r_xn[c][t]])
                else:
                    yb_ = ycnt_f[0] % 4
                    ycnt_f[0] += 1
                    p.op("dve", lambda e, c=c, gs=gs, yb_=yb_: e.scalar_tensor_tensor(
                        out=yst[yb_][:, :n], in0=xT[:, c, t0:t0 + n], scalar=gs, in1=rstd[:, :n], op0=ALU.mult, op1=ALU.mult),
                        reads=[r_xT[c][t], r_rstd, r_pv], writes=[r_yst[yb_]])
                    p.dma("sp", lambda e, c=c, yb_=yb_: e.dma_start(out=yT_o[c * 128:(c + 1) * 128, t0:t0 + n], in_=yst[yb_][:, :n]),
                          r_yst[yb_], reads=[r_yst[yb_]])

        def tail_tiles(tile_fn, nxt):
            gidx, final = nxt
            for t in range(5):
                tile_fn(t)
                if t >= 1:
                    norm_tile(t - 1, gidx, final=final)
            norm_tile(4, gidx, final=final)

        def ffn_jobs():
            jobs = []
            for gi, (f0, G) in enumerate(GROUPS):
                for fl in range(G):
                    if gi > 0 and fl == 0:
                        continue
                    jobs.append(("gu", gi, fl))
                if gi + 1 < len(GROUPS):
                    jobs.append(("gu", gi + 1, 0))
                jobs.append(("down", gi, None))
            return jobs

        def add_ffn_w(i):
            for kind, gi, fl in ffn_jobs():
                f0, G = GROUPS[gi]
                if kind == "gu":
                    wq.add("A", [ld_full(wgu_in[i, f0 + fl])])
                else:
                    for f in range(f0, f0 + G):
                        wq.add("B", [ld_full(wd_in[i, f * 128:(f + 1) * 128, :])])

        def ffn(i, nxt):
            cv = Carver()
            cv.off = BASE
            hT = [cv.bf16(6 * NT).rearrange("p (c n) -> p c n", c=6), cv.bf16(4 * NT).rearrange("p (c n) -> p c n", c=4)]
            r_h = [p.res(f"h{b}") for b in range(2)]
            sg = [cv.bf16(512) for _ in range(2)]
            r_sg = [p.res(f"sg{b}") for b in range(2)]
            ffn_sq8 = cv.bf16(DC * 512)
            cnt = [0]
            ycnt = [0]

            def gu(gi, fl):
                hb = gi % 2
                it = wq.next()
                w = it.ap.rearrange("p (m c j) -> p m c j", m=2, c=DC)
                for t, (t0, n) in enumerate(TT):
                    b = cnt[0] % 2
                    cnt[0] += 1
                    gps, ups = PS[b], PS[2 + b]
                    for c in range(DC):
                        mm(gps[:, :n], w[:, 0, c, :], xn[:, c, t0:t0 + n], c == 0, c == DC - 1, [it.res, r_xn[c][t]], [r_ps[b]])
                    for c in range(DC):
                        mm(ups[:, :n], w[:, 1, c, :], xn[:, c, t0:t0 + n], c == 0, c == DC - 1, [it.res, r_xn[c][t]], [r_ps[2 + b]])
                    p.op("act", lambda e, b=b, n=n, gps=gps: e.activation(out=sg[b][:, :n], in_=gps[:, :n], func=AF.Silu),
                         reads=[r_ps[b]], writes=[r_sg[b]])
                    p.op("dve", lambda e, b=b, n=n, t0=t0, ups=ups: e.tensor_tensor(
                        out=hT[hb][:, fl, t0:t0 + n], in0=ups[:, :n], in1=sg[b][:, :n], op=ALU.mult),
                        reads=[r_ps[2 + b], r_sg[b]], writes=[r_h[hb]])
                wq.release(it)

            def down_ct(wds, gi, c, t):
                hb = gi % 2
                G = GROUPS[gi][1]
                t0, n = TT[t]
                b = (4, 5, 7)[ycnt[0] % 3]
                ycnt[0] += 1
                yps = PS[b]
                for fl in range(G):
                    mm(yps[:, :n], wds[fl].ap[:, c * 128:(c + 1) * 128], hT[hb][:, fl, t0:t0 + n],
                       fl == 0, fl == G - 1, [wds[fl].res, r_h[hb]], [r_ps[b]])
                p.op("dve", lambda e: e.scalar_tensor_tensor(
                    out=xT[:, c, t0:t0 + n], in0=yps[:, :n], scalar=0.5, in1=xT[:, c, t0:t0 + n], op0=ALU.mult, op1=ALU.add),
                    reads=[r_ps[b], r_xT[c][t]], writes=[r_xT[c][t]])

            for kind, gi, fl in ffn_jobs():
                if kind == "gu":
                    gu(gi, fl)
                    continue
                G = GROUPS[gi][1]
                wds = [wq.next() for _ in range(G)]
                if gi < len(GROUPS) - 1:
                    for c in range(DC):
                        for t in range(5):
                            down_ct(wds, gi, c, t)
                else:
                    def tile_fn(t, wds=wds, gi=gi):
                        for c in range(DC):
                            down_ct(wds, gi, c, t)
                    set_sq8(ffn_sq8)
                    tail_tiles(tile_fn, nxt)
                for it in wds:
                    wq.release(it)
            p.barrier()

        def project(its, srcs, r_srcs, bank0=4, nxt=None):
            cntp = [0]
            K = len(its)

            def one(c, t):
                t0, n = TT[t]
                b = ((6, 7, 5) if bank0 == 6 else (4, 5, 7))[cntp[0] % 3]
                cntp[0] += 1
                yps = PS[b]
                for k in range(K):
                    mm(yps[:, :n], its[k].ap[:, c * 128:(c + 1) * 128], srcs[k][:, t0:t0 + n], k == 0, k == K - 1,
                       [its[k].res] + r_srcs, [r_ps[b]])
                p.op("dve", lambda e: e.tensor_tensor(out=xT[:, c, t0:t0 + n], in0=yps[:, :n], in1=xT[:, c, t0:t0 + n], op=ALU.add),
                     reads=[r_ps[b], r_xT[c][t]], writes=[r_xT[c][t]])

            if nxt is None:
                for c in range(DC):
                    for t in range(5):
                        one(c, t)
            else:
                def tile_fn(t):
                    for c in range(DC):
                        one(c, t)
                tail_tiles(tile_fn, nxt)

        def even_mixer(nxt):
            cv = Carver()
            base_off = BASE
            cv.off = base_off

            yaT = cv.bf16(4 * NT).rearrange("p (c n) -> p c n", c=4)
            r_ya = p.res("ya")
            uT = [cv.f32(2 + SEQ + 2 + NS) for _ in range(2)]
            r_u = [p.res(f"u{b}") for b in range(2)]
            xa_s = [cv.f32(512) for _ in range(2)]
            r_xa = [p.res(f"xa{b}") for b in range(2)]
            acc = [cv.f32(512) for _ in range(2)]
            r_acc = [p.res(f"acc{b}") for b in range(2)]
            cnt = 0
            fetched = {}

            def chunkw(flat):
                s = flat // 2
                if s not in fetched:
                    fetched[s] = wq.next()
                it = fetched[s]
                return it, it.ap.rearrange("p (m c j) -> p m c j", m=2, c=DC)[:, flat % 2]

            def rel_upto(flat):
                for s in list(fetched):
                    if 2 * s + 1 <= flat and not fetched[s].done:
                        wq.release(fetched[s])

            for cc in range(4):
                ub = uT[cc % 2]
                r_ub = r_u[cc % 2]
                p.op("dve", lambda e, ub=ub: e.memset(ub[:, 0:2], 0.0), writes=[r_ub])
                p.dma("sp", lambda e, ub=ub, cc=cc: e.dma_start(out=ub[:, 2 + SEQ:4 + SEQ], in_=ssc_in[cc * 128:(cc + 1) * 128, :]),
                      r_ub, writes=[r_ub])
                it_xa, w_xa = chunkw(3 * cc)
                it_gc, w_gc = chunkw(3 * cc + 1)
                it_gb, w_gb = chunkw(3 * cc + 2)
                for t, (t0, n) in enumerate(TT):
                    b = cnt % 2
                    cnt += 1
                    pxa, pgc, pgb = PS[b], PS[2 + b], PS[4 + b]
                    for c in range(DC):
                        mm(pxa[:, :n], w_xa[:, c, :], xn[:, c, t0:t0 + n], c == 0, c == DC - 1, [it_xa.res, r_xn[c][t]], [r_ps[b]])
                    for c in range(DC):
                        mm(pgc[:, :n], w_gc[:, c, :], xn[:, c, t0:t0 + n], c == 0, c == DC - 1, [it_gc.res, r_xn[c][t]], [r_ps[2 + b]])
                    for c in range(DC):
                        mm(pgb[:, :n], w_gb[:, c, :], xn[:, c, t0:t0 + n], c == 0, c == DC - 1, [it_gb.res, r_xn[c][t]], [r_ps[4 + b]])
                    u0 = 2 + t0 if t < 4 else 4 + SEQ
                    p.op("act", lambda e, b=b, n=n, pxa=pxa: e.activation(out=xa_s[b][:, :n], in_=pxa[:, :n], func=AF.Copy),
                         reads=[r_ps[b]], writes=[r_xa[b]])
                    p.op("dve", lambda e, b=b, n=n, u0=u0, ub=ub, pgc=pgc: e.tensor_tensor(
                        out=ub[:, u0:u0 + n], in0=pgc[:, :n], in1=xa_s[b][:, :n], op=ALU.mult),
                        reads=[r_ps[2 + b], r_xa[b]], writes=[r_ub])
                    cw = [pv[:, PV_CW + j * 4 + cc:PV_CW + j * 4 + cc + 1] for j in range(3)]
                    p.op("dve", lambda e, b=b, n=n, u0=u0, ub=ub, cw=cw: e.tensor_scalar(
                        out=acc[b][:, :n], in0=ub[:, u0 - 2:u0 - 2 + n], scalar1=cw[0], scalar2=None, op0=ALU.mult),
                        reads=[r_ub, r_pv], writes=[r_acc[b]])
                    for j in (1, 2):
                        p.op("dve", lambda e, b=b, n=n, u0=u0, ub=ub, cw=cw, j=j: e.scalar_tensor_tensor(
                            out=acc[b][:, :n], in0=ub[:, u0 - 2 + j:u0 - 2 + j + n], scalar=cw[j], in1=acc[b][:, :n],
                            op0=ALU.mult, op1=ALU.add),
                            reads=[r_ub, r_pv, r_acc[b]], writes=[r_acc[b]])
                    p.op("dve", lambda e, b=b, n=n, t0=t0, cc=cc, pgb=pgb: e.tensor_tensor(
                        out=yaT[:, cc, t0:t0 + n], in0=pgb[:, :n], in1=acc[b][:, :n], op=ALU.mult),
                        reads=[r_ps[4 + b], r_acc[b]], writes=[r_ya])
                rel_upto(3 * cc + 2)
                p.dma("sp", lambda e, ub=ub, cc=cc: e.dma_start(out=scp_o[cc * 128:(cc + 1) * 128, :], in_=ub[:, SEQ:SEQ + 2]),
                      r_ub, reads=[r_ub])
                p.dma("sp", lambda e, ub=ub, cc=cc: e.dma_start(out=scs_o[cc * 128:(cc + 1) * 128, :],
                                                               in_=ub[:, 2 + SEQ + NS:4 + SEQ + NS]),
                      r_ub, reads=[r_ub])
            for s in list(fetched):
                if not fetched[s].done:
                    wq.release(fetched[s])
            wo = [wq.next() for _ in range(4)]
            project(wo, [yaT[:, k, :] for k in range(4)], [r_ya], bank0=6)
            for it in wo:
                wq.release(it)
            p.barrier()

            mark(p, 'em_V')
            EM = int(os.environ.get("MK_EM", "9"))
            if EM <= 1:
                return
            cv.off = base_off
            Vb = cv.bf16(17 * 512).rearrange("p (j n) -> p j n", j=17)
            r_V = p.res("V")
            vst = [cv.f32(512) for _ in range(2)]
            r_vst = [p.res(f"vst{b}") for b in range(2)]
            lf = cv.f32(17 * 8)
            r_lf = p.res("lf")
            plf = cv.f32(16 * 8)
            r_plf = p.res("plf")
            cP = cv.f32(16 * 8)
            cS = cv.f32(17 * 8)
            r_c = p.res("c")
            biasP = cv.f32(40 * 8)
            biasS = cv.f32(17 * 8)
            r_bias = p.res("bias")
            off_ck = cv.off
            cKT = cv.bf16(SEQ)
            cVb = cv.bf16(16 * 128).rearrange("p (j n) -> p j n", j=16)
            em_sq8 = arena[:, off_ck:off_ck + DC * 256].bitcast(BF16)
            wv = [wq.next() for _ in range(8)]
            p.dma("sp", lambda e: e.dma_start(out=plf, in_=clf_in), r_plf, writes=[r_plf])
            flps = PS[7]
            vst_x = [arena[:, off_ck + 512 * b_:off_ck + 512 * (b_ + 1)] for b_ in range(2)]
            r_vst_x = [p.res(f"vstx{b_}") for b_ in range(2)]
            vst4 = [vst[0], vst[1], vst_x[0], vst_x[1]]
            r_vst4 = [r_vst[0], r_vst[1], r_vst_x[0], r_vst_x[1]]
            for j, (t0, n) in enumerate(TK):
                b = j % 2
                vps = PS[b]
                for c in range(DC):
                    mm(vps[:n, :], xn[:, c, t0:t0 + n], wv[c].ap[:, 0:512], c == 0, c == DC - 1,
                       [wv[c].res, r_xn[c][min(j // 4, 4)]], [r_ps[b]])
                for c in range(DC):
                    mm(flps[:n, j * 8:(j + 1) * 8], xn[:, c, t0:t0 + n], wv[c].ap[:, 512:520], c == 0, c == DC - 1,
                       [wv[c].res, r_xn[c][min(j // 4, 4)]], [r_ps[7]])
                sb4 = j % 4
                p.op("act", lambda e, sb4=sb4, n=n, vps=vps: e.activation(out=vst4[sb4][:n, :], in_=vps[:n, :], func=AF.Copy),
                     reads=[r_ps[b]], writes=[r_vst4[sb4]])
                p.op("dve", lambda e, b=b, n=n, j=j, vps=vps: e.tensor_copy(out=Vb[:n, j, :], in_=vps[:n, :]),
                     reads=[r_ps[b]], writes=[r_V])
                p.dma("sp", lambda e, sb4=sb4, n=n, t0=t0: e.dma_start(out=v_o[t0:t0 + n, :], in_=vst4[sb4][:n, :]), r_vst4[sb4],
                      reads=[r_vst4[sb4]])
            for it in wv:
                wq.release(it)
            if EM <= 2:
                p.barrier()
                return
            p.op("dve", lambda e: e.memset(lf[:, 128:136], 0.0), writes=[r_lf])
            p.op("dve", lambda e: e.tensor_tensor(out=lf[:, 0:128], in0=flps[:, 0:128], in1=pbc[:, PB_BF:PB_BF + 128], op=ALU.add),
                 reads=[r_ps[7], r_pv], writes=[r_lf])
            p.op("dve", lambda e: e.tensor_tensor(out=lf[0:16, 128:136], in0=flps[0:16, 128:136],
                                                  in1=pbc[0:16, PB_BF + 128:PB_BF + 136], op=ALU.add),
                 reads=[r_ps[7], r_pv], writes=[r_lf])
            p.op("act", lambda e: e.activation(out=lf[:], in_=lf[:], func=AF.Exp, scale=-1.0), reads=[r_lf], writes=[r_lf])
            p.op("act", lambda e: e.activation(out=lf[:], in_=lf[:], func=AF.Ln, bias=1.0), reads=[r_lf], writes=[r_lf])
            p.op("dve", lambda e: e.tensor_scalar(out=lf[:], in0=lf[:], scalar1=-1.0, scalar2=None, op0=ALU.mult),
                 reads=[r_lf], writes=[r_lf])
            p.dma("sp", lambda e: e.dma_start(out=lf_o, in_=lf), r_lf, reads=[r_lf])
            if EM <= 3:
                p.barrier()
                return
            def emit_cumsum():
                mark(p, 'em_cumsum')
                Pp = cv.f32(16 * 8)
                Ps = cv.f32(17 * 8)
                r_pp = p.res("pp")
                p.op("dve", lambda e: e.memset(Pp[:, 0:8], 0.0), writes=[r_pp])
                p.op("dve", lambda e: e.memset(Ps[:, 0:8], 0.0), writes=[r_pp])
                for j in range(15):
                    p.op("dve", lambda e, j=j: e.tensor_tensor(out=Pp[:, (j + 1) * 8:(j + 2) * 8], in0=Pp[:, j * 8:(j + 1) * 8],
                                                               in1=lf[:, j * 8:(j + 1) * 8], op=ALU.add), reads=[r_lf, r_pp], writes=[r_pp])
                for j in range(16):
                    p.op("dve", lambda e, j=j: e.tensor_tensor(out=Ps[:, (j + 1) * 8:(j + 2) * 8], in0=Ps[:, j * 8:(j + 1) * 8],
                                                               in1=plf[:, j * 8:(j + 1) * 8], op=ALU.add), reads=[r_plf, r_pp], writes=[r_pp])
                cps = PS[6]
                for j in range(16):
                    mm(cps[:, j * 8:(j + 1) * 8], tri_f[:], lf[:, j * 8:(j + 1) * 8], True, j == 0, [r_lf, r_const], [r_ps[6]])
                    if j > 0:
                        mm(cps[:, j * 8:(j + 1) * 8], ones_f[:], Pp[:, j * 8:(j + 1) * 8], False, True, [r_pp, r_const], [r_ps[6]])
                p.op("dve", lambda e: e.tensor_copy(out=cP[:], in_=cps[:, 0:128]), reads=[r_ps[6]], writes=[r_c])
                cps2 = PS[5]
                for j in range(16):
                    mm(cps2[:, j * 8:(j + 1) * 8], tri_f[:], plf[:, j * 8:(j + 1) * 8], True, j == 0, [r_plf, r_const], [r_ps[5]])
                    if j > 0:
                        mm(cps2[:, j * 8:(j + 1) * 8], ones_f[:], Ps[:, j * 8:(j + 1) * 8], False, True, [r_pp, r_const], [r_ps[5]])
                mm(cps2[:16, 128:136], tri_f[0:16, 0:16], lf[0:16, 128:136], True, False, [r_lf, r_const], [r_ps[5]])
                mm(cps2[:16, 128:136], ones_f[:, 0:16], Ps[:, 128:136], False, True, [r_pp, r_const], [r_ps[5]])
                p.op("dve", lambda e: e.tensor_copy(out=cS[:, 0:128], in_=cps2[:, 0:128]), reads=[r_ps[5]], writes=[r_c])
                p.op("dve", lambda e: e.tensor_copy(out=cS[0:16, 128:136], in_=cps2[0:16, 128:136]), reads=[r_ps[5]], writes=[r_c])
                eps_ = PS[4]
                for Q in range(4):
                    j = 4 * Q + 3
                    mm(eps_[:, Q * 8:(Q + 1) * 8], esel[:, 0, :], cP[:, j * 8:(j + 1) * 8], True, True, [r_c, r_const], [r_ps[4]])
                mm(eps_[:, 32:40], esel[0:16, 1, :], cS[0:16, 128:136], True, True, [r_c, r_const], [r_ps[4]])
                bidx = {}
                nb = 0
                for Q in range(4):
                    for j in range(4 * Q + 4):
                        bidx[(Q, j)] = nb
                        p.op("dve", lambda e, Q=Q, j=j, nb=nb: e.tensor_tensor(
                            out=biasP[:, nb * 8:(nb + 1) * 8], in0=eps_[:, Q * 8:(Q + 1) * 8], in1=cP[:, j * 8:(j + 1) * 8],
                            op=ALU.subtract), reads=[r_ps[4], r_c], writes=[r_bias])
                        nb += 1
                for j in range(17):
                    nj = 128 if j < 16 else 16
                    p.op("dve", lambda e, j=j, nj=nj: e.tensor_tensor(
                        out=biasS[:nj, j * 8:(j + 1) * 8], in0=eps_[:nj, 32:40], in1=cS[:nj, j * 8:(j + 1) * 8],
                        op=ALU.subtract), reads=[r_ps[4], r_c], writes=[r_bias])
                return bidx

            if EM <= 4:
                p.barrier()
                return
            mark(p, 'em_attn')
            Qz = [cv.bf16(NT), cv.bf16(NT)]
            KT = cv.bf16(NT)
            r_qk = p.res("qk")
            p.op("dve", lambda e: e.memset(Qz[0][64:128, :], 0.0), writes=[r_qk])
            p.op("dve", lambda e: e.memset(Qz[1][0:64, :], 0.0), writes=[r_qk])
            r_ck = p.res("ck")
            k_ck = r_ck
            ybT = cv.bf16(NT)
            r_ybt = [p.res(f"yb{t}") for t in range(5)]
            kst, r_kst = vst, r_vst
            pT = [cv.bf16(512) for _ in range(3)]
            r_pT = [p.res(f"pT{b}") for b in range(3)]
            rl, r_rl = vst[1], r_vst[1]
            pcnt = [0]
            scnt = [0]

            pending_cb = []

            def attn_run(groups, tile_done, flush):
                steps = []
                for gi, (hh, q0, nq, ktiles) in enumerate(groups):
                    for idx, kt in enumerate(ktiles):
                        steps.append((gi, idx, len(ktiles), hh, q0, nq, kt))

                def emit_qk(si):
                    gi, idx, nkt, hh, q0, nq, (kt_ap, v_ap, b_ap, nk, n0, rr) = steps[si]
                    rows = slice(hh * 64, hh * 64 + 64)
                    sb_ = (2, 3, 1)[si % 3]
                    sps = PS[sb_]
                    diag = n0 is not None
                    n0_ = n0 if diag else 0
                    N = nq - n0_
                    dn = min(128, N) if diag else 0
                    QH = Qz[hh]
                    if diag:
                        mm(sps[:nk, 0:dn], ident_b[:nk, :nk], maskneg[:nk, 0:dn], True, False, [r_const], [r_ps[sb_]])
                        mm(sps[:nk, 0:dn], kt_ap, QH[:, q0 + n0_:q0 + n0_ + dn], False, True, [r_qk] + rr, [r_ps[sb_]])
                        if N > dn:
                            mm(sps[:nk, dn:N], kt_ap, QH[:, q0 + n0_ + dn:q0 + nq], True, True, [r_qk] + rr, [r_ps[sb_]])
                    else:
                        mm(sps[:nk, 0:N], kt_ap, QH[:, q0:q0 + nq], True, True, [r_qk] + rr, [r_ps[sb_]])

                def emit_rest(si):
                    gi, idx, nkt, hh, q0, nq, (kt_ap, v_ap, b_ap, nk, n0, rr) = steps[si]
                    rows = slice(hh * 64, hh * 64 + 64)
                    sb_ = (2, 3, 1)[si % 3]
                    sps = PS[sb_]
                    obank = 4 + 2 * (gi % 2)
                    ops_, lps_ = PS[obank], PS[obank + 1]
                    n0_ = n0 if n0 is not None else 0
                    N = nq - n0_
                    pb_ = si % 3
                    p.op("act", lambda e: e.activation(out=pT[pb_][:nk, 0:N], in_=sps[:nk, 0:N], func=AF.Exp, bias=b_ap, scale=0.125),
                         reads=[r_ps[sb_], r_bias], writes=[r_pT[pb_]])
                    mm(ops_[:, n0_:nq], v_ap, pT[pb_][:nk, 0:N], idx == 0, idx == nkt - 1, [r_pT[pb_], r_V] + rr, [r_ps[obank]])
                    mm(lps_[:, n0_:nq], ones_b[:nk, :], pT[pb_][:nk, 0:N], idx == 0, idx == nkt - 1, [r_pT[pb_], r_const],
                       [r_ps[obank + 1]])
                    if idx == nkt - 1:
                        p.op("dve", lambda e: e.reciprocal(out=rl[rows, 0:nq], in_=lps_[rows, 0:nq]),
                             reads=[r_ps[obank + 1]], writes=[r_rl])
                        p.op("dve", lambda e: e.tensor_tensor(out=ybT[rows, q0:q0 + nq], in0=ops_[rows, 0:nq], in1=rl[rows, 0:nq], op=ALU.mult),
                             reads=[r_ps[obank], r_rl], writes=[r_ybt[min(q0 // 512, 4)]])
                        if gi in tile_done:
                            for k_, job in enumerate(tile_done[gi]):
                                pending_cb.append((si + 6 + k_, job))

                for k_ in range(len(pending_cb)):
                    pending_cb[k_] = (-1, pending_cb[k_][1])
                emit_qk(0)
                if len(steps) > 1:
                    emit_qk(1)
                for si in range(len(steps)):
                    if si + 2 < len(steps):
                        emit_qk(si + 2)
                    emit_rest(si)
                    if pending_cb and pending_cb[0][0] <= si:
                        pending_cb.pop(0)[1]()
                if flush:
                    while pending_cb:
                        pending_cb.pop(0)[1]()

            for pr in range(4):
                itA = wq.next()
                itO = wq.next()
                wA = itA.ap.rearrange("p (m c j) -> p m c j", m=2, c=DC)
                p.dma("pool", lambda e, pr=pr: e.dma_start(out=cKT[:, :], in_=ckt_in[pr * 128:(pr + 1) * 128, :]), k_ck,
                      writes=[r_ck] + (r_vst_x if pr == 0 else []))
                p.dma("pool", lambda e, pr=pr: e.dma_start(
                    out=cVb.rearrange("p j n -> p (j n)"), in_=cv_in[pr]), k_ck, writes=[r_ck])
                for t, (t0, n) in enumerate(TT):
                    b = t % 2
                    pq, pk = PS[b], PS[6 + b]
                    for c in range(DC):
                        mm(pq[:, :n], wA[:, 0, c, :], xn[:, c, t0:t0 + n], c == 0, c == DC - 1, [itA.res, r_xn[c][t]], [r_ps[b]])
                    for c in range(DC):
                        mm(pk[:, :n], wA[:, 1, c, :], xn[:, c, t0:t0 + n], c == 0, c == DC - 1, [itA.res, r_xn[c][t]], [r_ps[6 + b]])
                    p.op("act", lambda e, n=n, t0=t0, pq=pq: e.activation(out=Qz[0][0:64, t0:t0 + n], in_=pq[0:64, :n], func=AF.Copy),
                         reads=[r_ps[b]], writes=[r_qk])
                    p.op("act", lambda e, n=n, t0=t0, pq=pq: e.activation(out=Qz[1][64:128, t0:t0 + n], in_=pq[64:128, :n], func=AF.Copy),
                         reads=[r_ps[b]], writes=[r_qk])
                    p.op("act", lambda e, n=n, t0=t0, pk=pk: e.activation(out=KT[:, t0:t0 + n], in_=pk[:, :n], func=AF.Copy),
                         reads=[r_ps[6 + b]], writes=[r_qk])
                    p.op("dve", lambda e, n=n, b=b, pk=pk: e.tensor_copy(out=kst[b][:, :n], in_=pk[:, :n]),
                         reads=[r_ps[6 + b]], writes=[r_kst[b]])
                    p.dma("sp", lambda e, n=n, b=b, t0=t0, pr=pr: e.dma_start(
                        out=kT_o[pr * 128:(pr + 1) * 128, t0:t0 + n], in_=kst[b][:, :n]), r_kst[b], reads=[r_kst[b]])
                wq.release(itA)
                if pr == 0:
                    bidx = emit_cumsum()
                def prompt_group(hh, Q):
                    h = 2 * pr + hh
                    kts = []
                    for j in range(4 * Q + 4):
                        n0 = (j - 4 * Q) * 128 if j >= 4 * Q else None
                        bi = bidx[(Q, j)]
                        kts.append((KT[:, j * 128:(j + 1) * 128], Vb[:, j, pr * 128:(pr + 1) * 128],
                                    biasP[:, bi * 8 + h:bi * 8 + h + 1], 128, n0, []))
                    return (hh, Q * 512, 512, kts)

                def sample_group(hh):
                    h = 2 * pr + hh
                    kts = []
                    for j in range(16):
                        kts.append((cKT[:, j * 128:(j + 1) * 128], cVb[:, j, :], biasS[:, j * 8 + h:j * 8 + h + 1],
                                    128, None, [r_ck]))
                    kts.append((KT[:, SEQ:NT], Vb[0:16, 16, pr * 128:(pr + 1) * 128],
                                biasS[0:16, 128 + h:128 + h + 1], 16, 0, []))
                    return (hh, SEQ, NS, kts)

                groups = [sample_group(0), sample_group(1)]
                tiles_order = [4]
                for Q in range(4):
                    groups += [prompt_group(0, Q), prompt_group(1, Q)]
                    tiles_order.append(Q)
                last = pr == 3
                if last:
                    set_sq8(em_sq8)
                prev = [None]

                def proj_units(t, itO=itO, last=last):
                    t0, n = TT[t]
                    jobs = []
                    for c in range(DC):
                        def unit(c=c):
                            b = 0
                            mm(PS[b][:, :n], itO.ap[:, c * 128:(c + 1) * 128], ybT[:, t0:t0 + n], True, True, [itO.res, r_ybt[t]], [r_ps[b]])
                            p.op("dve", lambda e: e.tensor_tensor(out=xT[:, c, t0:t0 + n], in0=PS[b][:, :n], in1=xT[:, c, t0:t0 + n], op=ALU.add),
                                 reads=[r_ps[b], r_xT[c][t]], writes=[r_xT[c][t]])
                        jobs.append(unit)
                    if last:
                        def nrm(t=t):
                            if prev[0] is not None:
                                norm_tile(prev[0], nxt[0], ssb=0, final=nxt[1])
                            prev[0] = t
                        jobs.append(nrm)
                    if t == 3:
                        jobs.append(lambda itO=itO: wq.release(itO))
                    return jobs

                tile_done = {2 * i + 1: proj_units(t) for i, t in enumerate(tiles_order)}
                attn_run(groups, tile_done, flush=last)
                if last:
                    norm_tile(prev[0], nxt[0], ssb=0, final=nxt[1])
            p.barrier()

        def odd_mixer(nxt):
            cv = Carver()
            base_off = BASE
            cv.off = base_off

            vn = cv.bf16(17 * 512).rearrange("p (j n) -> p j n", j=17)
            r_vn = p.res("vn")
            uT = cv.bf16(4 * NT).rearrange("p (c n) -> p c n", c=4)
            r_uT = p.res("uT")
            gvf = [cv.f32(512) for _ in range(4)]
            r_gvf = [p.res(f"gvf{b}") for b in range(4)]
            stats = [cv.f32(8) for _ in range(4)]
            r_st = [p.res(f"st{b}") for b in range(4)]
            mv4 = cv.f32(8).rearrange("p (q k) -> p q k", k=2)
            r_mv4 = [p.res(f"mv{b}") for b in range(2)]
            bsr = cv.f32(1024)
            r_bsr = p.res("bsr")
            p.dma("sp", lambda e: e.dma_start(out=bsr[0:1, :], in_=bs_in), r_bsr, writes=[r_bsr])
            bs_hi = cv.bf16(1024)
            bs_lo = cv.bf16(1024)
            bs_t = cv.f32(1024)
            r_bs2 = p.res("bs2")
            p.op("dve", lambda e: e.tensor_copy(out=bs_hi[0:1, :], in_=bsr[0:1, :]), reads=[r_bsr], writes=[r_bs2])
            p.op("dve", lambda e: e.tensor_copy(out=bs_t[0:1, :], in_=bs_hi[0:1, :]), reads=[r_bs2], writes=[r_bs2])
            p.op("dve", lambda e: e.tensor_tensor(out=bs_t[0:1, :], in0=bsr[0:1, :], in1=bs_t[0:1, :], op=ALU.subtract),
                 reads=[r_bsr, r_bs2], writes=[r_bs2])
            p.op("dve", lambda e: e.tensor_copy(out=bs_lo[0:1, :], in_=bs_t[0:1, :]), reads=[r_bs2], writes=[r_bs2])
            wv = [wq.next() for _ in range(8)]
            itU = [wq.next() for _ in range(2)]
            ujobs = [(cc, t) for cc in range(4) for t in range(5)]
            ucnt = [0]

            def u_job():
                if ucnt[0] >= len(ujobs):
                    return
                cc, t = ujobs[ucnt[0]]
                t0, n = TT[t]
                b = 2 + ucnt[0] % 2
                ucnt[0] += 1
                wu_ = itU[cc // 2].ap.rearrange("p (m c j) -> p m c j", m=2, c=DC)[:, cc % 2]
                for c in range(DC):
                    mm(PS[b][:, :n], wu_[:, c, :], xn[:, c, t0:t0 + n], c == 0, c == DC - 1, [itU[cc // 2].res, r_xn[c][t]], [r_ps[b]])
                p.op("act", lambda e: e.activation(out=uT[:, cc, t0:t0 + n], in_=PS[b][:, :n], func=AF.Gelu_apprx_tanh),
                     reads=[r_ps[b]], writes=[r_uT])

            def vv_finish(js):
                q0 = js[0] % 4
                nq_ = len(js)
                ps_ = (js[0] // 2) % 2
                nmax = TK[js[0]][1]
                p.op("act", lambda e: e.activation(out=mv4[:nmax, q0:q0 + nq_, 1:2], in_=mv4[:nmax, q0:q0 + nq_, 1:2], func=AF.Sqrt, bias=EPS, scale=1.0),
                     reads=[r_mv4[ps_]], writes=[r_mv4[ps_]])
                p.op("dve", lambda e: e.reciprocal(out=mv4[:nmax, q0:q0 + nq_, 1:2], in_=mv4[:nmax, q0:q0 + nq_, 1:2]),
                     reads=[r_mv4[ps_]], writes=[r_mv4[ps_]])
                for j in js:
                    t0, n = TK[j]
                    q = j % 4
                    g_ = gvf[q]
                    p.op("dve", lambda e, n=n, q=q, g_=g_: e.tensor_scalar(
                        out=g_[:n, :], in0=g_[:n, :], scalar1=mv4[:n, q, 0:1], scalar2=mv4[:n, q, 1:2], op0=ALU.subtract, op1=ALU.mult),
                        reads=[r_mv4[ps_], r_gvf[q]], writes=[r_gvf[q]])
                    p.op("dve", lambda e, n=n, g_=g_: e.tensor_tensor(out=g_[:n, :], in0=g_[:n, :], in1=pbc[:n, PB_LNG:PB_LNG + 512], op=ALU.mult),
                         reads=[r_gvf[q], r_pv], writes=[r_gvf[q]])
                    if j == 16:
                        p.op("dve", lambda e, n=n, g_=g_: e.tensor_tensor(out=g_[:n, :], in0=g_[:n, :], in1=pbc[:n, PB_LNB:PB_LNB + 512], op=ALU.add),
                             reads=[r_gvf[q], r_pv], writes=[r_gvf[q]])
                        p.op("act", lambda e, n=n, g_=g_, j=j: e.activation(out=vn[:n, j, :], in_=g_[:n, :], func=AF.Copy),
                             reads=[r_gvf[q]], writes=[r_vn])
                        p.dma("sp", lambda e, g_=g_: e.dma_start(out=gv_o[:, :], in_=g_[0:NS, :]), r_gvf[q], reads=[r_gvf[q]])
                    else:
                        p.op("dve", lambda e, n=n, g_=g_, j=j: e.tensor_tensor(out=vn[:n, j, :], in0=g_[:n, :], in1=pbc[:n, PB_LNB:PB_LNB + 512], op=ALU.add),
                             reads=[r_gvf[q], r_pv], writes=[r_vn])

            for j, (t0, n) in enumerate(TK):
                u_job()
                b = j % 2
                q = j % 4
                ps_ = (j // 2) % 2
                vps = PS[b]
                for c in range(DC):
                    mm(vps[:n, :], xn[:, c, t0:t0 + n], wv[c].ap[:, 0:512], c == 0, c == DC - 1,
                       [wv[c].res, r_xn[c][min(j // 4, 4)]], [r_ps[b]])
                g_ = gvf[q]
                p.op("act", lambda e, n=n, g_=g_, vps=vps: e.activation(out=g_[:n, :], in_=vps[:n, :], func=AF.Gelu_apprx_tanh),
                     reads=[r_ps[b]], writes=[r_gvf[q]])
                p.op("dve", lambda e, n=n, g_=g_, q=q: e.bn_stats(out=stats[q][:n, 0:6], in_=g_[:n, :]),
                     reads=[r_gvf[q]], writes=[r_st[q]])
                p.op("dve", lambda e, n=n, q=q: e.bn_aggr(out=mv4[:n, q, :], in_=stats[q][:n, 0:6]),
                     reads=[r_st[q]], writes=[r_mv4[ps_]])
                if j % 2 == 1:
                    vv_finish([j - 1, j])
                elif j == 16:
                    vv_finish([16])
            while ucnt[0] < len(ujobs):
                u_job()
            for it in wv:
                wq.release(it)
            for it in itU:
                wq.release(it)
            mark(p, 'om_spatial')
            bsh3 = bs_hi[0:1, :].rearrange("p (g t) -> p g t", g=8)
            bsl3 = bs_lo[0:1, :].rearrange("p (g t) -> p g t", g=8)
            for pr in range(4):
                for t, (t0, n) in enumerate(TT):
                    banks = [(PS[4 + (t % 2) * 2], r_ps[4 + (t % 2) * 2]), (PS[5 + (t % 2) * 2], r_ps[5 + (t % 2) * 2])]
                    nch = 4 if t < 4 else 1
                    L = 128 if t < 4 else 16
                    for ch in range(nch):
                        j = t * 4 + ch
                        pp, rr = banks[ch // 2]
                        c0 = (ch % 2) * 2 * L
                        o3 = pp[:, c0:c0 + 2 * L].rearrange("p (g t) -> p g t", g=2)
                        mm(o3, vn[:L, j, pr * 128:(pr + 1) * 128], wstm[:L, 2 * pr:2 * pr + 2, 0:L], True, False, [r_vn, r_pv], [rr])
                        mm(o3, ones_b[0:1, :], bsh3[:, 2 * pr:2 * pr + 2, 0:L], False, False, [r_bs2, r_const], [rr])
                        mm(o3, ones_b[0:1, :], bsl3[:, 2 * pr:2 * pr + 2, 0:L], False, True, [r_bs2, r_const], [rr])
                    for bi_, (pp, rr) in enumerate(banks):
                        nb_ = min(2, nch - 2 * bi_)
                        if nb_ <= 0:
                            continue
                        cb = t0 + 2 * bi_ * L
                        for gi_, rows in enumerate((slice(0, 64), slice(64, 128))):
                            src = pp[rows, 0:nb_ * 2 * L].rearrange("p (ch g t) -> p ch g t", g=2, t=L)[:, :, gi_, :]
                            dst = uT[rows, pr, cb:cb + nb_ * L].rearrange("p (ch t) -> p ch t", t=L)
                            p.op("dve", lambda e, src=src, dst=dst: e.tensor_tensor(out=dst, in0=src, in1=dst, op=ALU.mult),
                                 reads=[rr, r_uT], writes=[r_uT])
            wo = [wq.next() for _ in range(4)]
            project(wo, [uT[:, k, :] for k in range(4)], [r_uT], bank0=6)
            for it in wo:
                wq.release(it)
            p.barrier()

            mark(p, 'om_D1')
            cv.off = base_off
            WP = 30 + SEQ
            WS = 30 + NS
            off_gbf = cv.off
            gbf = cv.bf16(4 * (WP + WS)).rearrange("p (c n) -> p c n", c=4)
            r_g = [p.res(f"g{c}") for c in range(4)]
            gtail = cv.f32(4 * 30).rearrange("p (c n) -> p c n", c=4)
            sctx = cv.f32(4 * 30).rearrange("p (c n) -> p c n", c=4)
            gs32 = cv.f32(4 * 16).rearrange("p (c n) -> p c n", c=4)
            r_gt = [p.res(f"gt{c}") for c in range(4)]
            sgb = [cv.f32(512) for _ in range(2)]
            r_sgb = [p.res(f"sgb{b}") for b in range(2)]
            off_d1 = cv.off
            dg = [cv.bf16(31 * 128).rearrange("p (j n) -> p j n", j=31) for _ in range(2)]
            r_dg = [p.res(f"dg{b}") for b in range(2)]
            off_d2 = cv.off

            def dg_op(cc, j):
                db = cc % 2
                p.op("act", lambda e: e.activation(
                    out=dg[db][:, j, :], in_=ident_f[:], func=AF.Copy, scale=pv[:, PV_DW + j * 4 + cc:PV_DW + j * 4 + cc + 1]),
                    reads=[r_const, r_pv], writes=[r_dg[db]])

            def build_dg(cc):
                for j in range(31):
                    dg_op(cc, j)

            dg_pending = [(cc, j) for cc in (0, 1) for j in range(31)]
            itG = [wq.next() for _ in range(4)]
            cnt = 0
            for cc in range(4):
                w2 = itG[cc].ap.rearrange("p (m c j) -> p m c j", m=2, c=DC)
                p.op("dve", lambda e, cc=cc: e.memset(gbf[:, cc, 0:30], 0.0), writes=[r_g[cc]])
                p.dma("sp", lambda e, cc=cc: e.dma_start(out=sctx[:, cc, :], in_=scc_in[cc * 128:(cc + 1) * 128, :]),
                      r_gt[cc], writes=[r_gt[cc]])
                p.op("dve", lambda e, cc=cc: e.tensor_copy(out=gbf[:, cc, WP:WP + 30], in_=sctx[:, cc, :]),
                     reads=[r_gt[cc]], writes=[r_g[cc]])
                for t, (t0, n) in enumerate(TT):
                    b = cnt % 2
                    cnt += 1
                    pga, pgb = PS[b], PS[2 + b]
                    for c in range(DC):
                        mm(pga[:, :n], w2[:, 0, c, :], xn[:, c, t0:t0 + n], c == 0, c == DC - 1, [itG[cc].res, r_xn[c][t]], [r_ps[b]])
                    for c in range(DC):
                        mm(pgb[:, :n], w2[:, 1, c, :], xn[:, c, t0:t0 + n], c == 0, c == DC - 1, [itG[cc].res, r_xn[c][t]], [r_ps[2 + b]])
                    g0 = 30 + t0 if t < 4 else WP + 30
                    p.op("act", lambda e, b=b, n=n, pgb=pgb: e.activation(out=sgb[b][:, :n], in_=pgb[:, :n], func=AF.Sigmoid),
                         reads=[r_ps[2 + b]], writes=[r_sgb[b]])
                    for _ in range(4):
                        if dg_pending:
                            dg_op(*dg_pending.pop(0))
                    p.op("dve", lambda e, b=b, n=n, cc=cc, g0=g0, pga=pga: e.tensor_tensor(
                        out=gbf[:, cc, g0:g0 + n], in0=pga[:, :n], in1=sgb[b][:, :n], op=ALU.mult),
                        reads=[r_ps[b], r_sgb[b]], writes=[r_g[cc]])
                    if t == 3:
                        p.op("dve", lambda e, b=b, cc=cc, pga=pga: e.tensor_tensor(
                            out=gtail[:, cc, :], in0=pga[:, 482:512], in1=sgb[b][:, 482:512], op=ALU.mult),
                            reads=[r_ps[b], r_sgb[b]], writes=[r_gt[cc]])
                    if t == 4:
                        p.op("dve", lambda e, b=b, cc=cc, pga=pga: e.tensor_tensor(
                            out=gs32[:, cc, :], in0=pga[:, 0:16], in1=sgb[b][:, 0:16], op=ALU.mult),
                            reads=[r_ps[b], r_sgb[b]], writes=[r_gt[cc]])
                wq.release(itG[cc])
                rs = slice(cc * 128, (cc + 1) * 128)
                p.dma("sp", lambda e, cc=cc, rs=rs: e.dma_start(out=ccp_o[rs, :], in_=gtail[:, cc, :]), r_gt[cc], reads=[r_gt[cc]])
                p.dma("sp", lambda e, cc=cc, rs=rs: e.dma_start(out=ccs_o[rs, 0:14], in_=sctx[:, cc, 16:30]), r_gt[cc], reads=[r_gt[cc]])
                p.dma("sp", lambda e, cc=cc, rs=rs: e.dma_start(out=ccs_o[rs, 14:30], in_=gs32[:, cc, :]), r_gt[cc], reads=[r_gt[cc]])
            while dg_pending:
                dg_op(*dg_pending.pop(0))
            p.barrier()
            mark(p, 'om_D2conv')
            cv.off = 0
            cdf = cv.f32(4 * NT).rearrange("p (c n) -> p c n", c=4)
            r_cd = [p.res(f"cd{t}") for t in range(5)]
            assert cv.off <= XN_W, (cv.off, XN_W)
            cnt = 0
            for cc in range(4):
                db = cc % 2
                dg_next = [(cc + 1, j) for j in range(31)] if 1 <= cc < 3 else []
                for t, (t0, n) in enumerate(TT):
                    b = cnt % 2
                    cnt += 1
                    g0 = 30 + t0 if t < 4 else WP + 30
                    for j in range(31):
                        mm(PS[b][:, :n], dg[db][:, j, :], gbf[:, cc, g0 - 30 + j:g0 - 30 + j + n], j == 0, j == 30,
                           [r_dg[db], r_g[cc]], [r_ps[b]])
                    p.op("act", lambda e, b=b, n=n, cc=cc, t0=t0: e.activation(
                        out=cdf[:, cc, t0:t0 + n], in_=PS[b][:, :n], func=AF.Identity, bias=pv[:, PV_DWB + cc:PV_DWB + cc + 1], scale=1.0),
                        reads=[r_ps[b], r_pv], writes=[r_cd[t]])
                    for _ in range(8):
                        if dg_next:
                            dg_op(*dg_next.pop(0))
                while dg_next:
                    dg_op(*dg_next.pop(0))
            p.barrier()
            mark(p, 'om_D3ln')
            cv.off = off_gbf
            ydT = cv.bf16(4 * NT).rearrange("p (c n) -> p c n", c=4)
            r_yd = p.res("yd")
            assert cv.off <= off_d1
            cv.off = off_d1
            sqd = [cv.bf16(512) for _ in range(2)]
            r_sqd = [p.res(f"sqd{b}") for b in range(2)]
            mu2 = [cv.f32(512) for _ in range(2)]
            ex22 = [cv.f32(512) for _ in range(2)]
            r_mu2 = [p.res(f"mu{b}") for b in range(2)]
            tmp = [cv.f32(512) for _ in range(2)]
            r_tmp = [p.res(f"tmp{b}") for b in range(2)]
            set_sq8(cv.bf16(DC * 512))

            def ln_stats(t):
                t0, n = TT[t]
                b = t % 2
                s1, s2 = PS[b], PS[2 + b]
                mu, ex2, r_mu = mu2[b], ex22[b], r_mu2[b]
                for cc in range(4):
                    mm(s1[:, :n], ones_f[:], cdf[:, cc, t0:t0 + n], cc == 0, cc == 3, [r_cd[t], r_const], [r_ps[b]])
                for cc in range(4):
                    sb2 = cc % 2
                    p.op("act", lambda e, cc=cc, sb2=sb2: e.activation(out=sqd[sb2][:, :n], in_=cdf[:, cc, t0:t0 + n], func=AF.Square),
                         reads=[r_cd[t]], writes=[r_sqd[sb2]])
                    mm(s2[:, :n], ones_b[:], sqd[sb2][:, :n], cc == 0, cc == 3, [r_sqd[sb2], r_const], [r_ps[2 + b]])
                p.op("act", lambda e: e.activation(out=mu[:, :n], in_=s1[:, :n], func=AF.Copy, scale=1.0 / 512),
                     reads=[r_ps[b]], writes=[r_mu])
                p.op("dve", lambda e: e.tensor_tensor(out=ex2[:, :n], in0=mu[:, :n], in1=mu[:, :n], op=ALU.mult),
                     reads=[r_mu], writes=[r_mu])
                p.op("dve", lambda e: e.scalar_tensor_tensor(out=ex2[:, :n], in0=s2[:, :n], scalar=1.0 / 512, in1=ex2[:, :n],
                                                             op0=ALU.mult, op1=ALU.subtract),
                     reads=[r_ps[2 + b], r_mu], writes=[r_mu])
                p.op("act", lambda e: e.activation(out=ex2[:, :n], in_=ex2[:, :n], func=AF.Ln, bias=EPS, scale=1.0),
                     reads=[r_mu], writes=[r_mu])
                p.op("act", lambda e: e.activation(out=ex2[:, :n], in_=ex2[:, :n], func=AF.Exp, scale=-0.5),
                     reads=[r_mu], writes=[r_mu])

            def ln_apply(t):
                t0, n = TT[t]
                b = t % 2
                mu, ex2, r_mu = mu2[b], ex22[b], r_mu2[b]
                for cc in range(4):
                    tb = cc % 2
                    p.op("dve", lambda e, cc=cc, tb=tb: e.tensor_tensor(out=tmp[tb][:, :n], in0=cdf[:, cc, t0:t0 + n], in1=mu[:, :n], op=ALU.subtract),
                         reads=[r_cd[t], r_mu], writes=[r_tmp[tb]])
                    p.op("dve", lambda e, tb=tb: e.tensor_tensor(out=tmp[tb][:, :n], in0=tmp[tb][:, :n], in1=ex2[:, :n], op=ALU.mult),
                         reads=[r_tmp[tb], r_mu], writes=[r_tmp[tb]])
                    p.op("act", lambda e, cc=cc, tb=tb: e.activation(
                        out=ydT[:, cc, t0:t0 + n], in_=tmp[tb][:, :n], func=AF.Silu,
                        scale=pv[:, PV_CNG + cc:PV_CNG + cc + 1], bias=pv[:, PV_CNB + cc:PV_CNB + cc + 1]),
                        reads=[r_tmp[tb], r_pv], writes=[r_yd])

            ln_stats(0)
            for t in range(5):
                if t + 1 < 5:
                    ln_stats(t + 1)
                ln_apply(t)
            wo = [wq.next() for _ in range(4)]
            project(wo, [ydT[:, k, :] for k in range(4)], [r_yd], bank0=4, nxt=nxt)
            for it in wo:
                wq.release(it)
            p.barrier()

        add_ffn_w(0)
        add_even_w()
        add_ffn_w(1)
        add_ffn_w(2)
        add_odd_w()
        add_ffn_w(3)
        set_sq8(arena[:, BASE + 10320 + 512:BASE + 10320 + 512 + DC * 256].bitcast(BF16))
        for t in range(5):
            norm_tile(t, 0)
        mark(p, 'ffn0')
        ffn(0, (1, False))
        mark(p, 'even')
        even_mixer((2, False))
        mark(p, 'ffn1')
        ffn(1, (3, False))
        mark(p, 'ffn2')
        ffn(2, (4, False))
        mark(p, 'odd')
        odd_mixer((5, False))
        mark(p, 'ffn3')
        ffn(3, (6, True))
        mark(p, 'end')
        p.emit(nc, st)
    return nc


_NC_CACHE = {}
MARKS = []


def mark(p, name):
    MARKS.append((name, len(p.streams['pe'])))


def _prep(x_prompt, x_sample, cache_fox_k, cache_fox_v, cache_fox_logf, state_sconv, state_cconv,
           ffn1_g, ffn1_wg, ffn1_wu, ffn1_wd, mix_g, ffn2_g, ffn2_wg, ffn2_wu, ffn2_wd,
           e_w_in, e_b_f, e_conv_w, e_w_out, o_w_in, o_ln_g, o_ln_b, o_ws, o_bs,
           o_dw, o_dw_b, o_cn_g, o_cn_b, o_w_out, final_g):
    f = lambda a: np.ascontiguousarray(np.asarray(a, dtype=np.float32))
    x_prompt, x_sample = f(x_prompt), f(x_sample)
    wg = [f(ffn1_wg)[0], f(ffn2_wg)[0], f(ffn1_wg)[1], f(ffn2_wg)[1]]
    wu = [f(ffn1_wu)[0], f(ffn2_wu)[0], f(ffn1_wu)[1], f(ffn2_wu)[1]]
    wd = np.stack([f(ffn1_wd)[0], f(ffn2_wd)[0], f(ffn1_wd)[1], f(ffn2_wd)[1]])
    wgu = np.empty((4, NF, 128, 2, 8, 128), np.float32)
    for i in range(4):
        wgu[i, :, :, 0] = wg[i].reshape(8, 128, NF, 128).transpose(2, 1, 0, 3)
        wgu[i, :, :, 1] = wu[i].reshape(8, 128, NF, 128).transpose(2, 1, 0, 3)
    wgu = wgu.reshape(4, NF, 128, 2048)

    def chunks(w, order):
        n = len(order)
        o = np.empty((n // 2, 128, 2, 8, 128), np.float32)
        for k, ch in enumerate(order):
            o[k // 2, :, k % 2] = w[:, ch * 128:(ch + 1) * 128].reshape(8, 128, 128).transpose(1, 0, 2)
        return o.reshape(n // 2, 128, 2048)

    ew = f(e_w_in)[0]
    ewa = chunks(ew, E_ORDER)
    ewv = np.ascontiguousarray(ew[:, 2560:3080].reshape(8, 128, 520).transpose(1, 0, 2))
    ow = f(o_w_in)[0]
    owa = chunks(ow, O_ORDER)
    owv = np.ascontiguousarray(ow[:, 512:1024].reshape(8, 128, 512).transpose(1, 0, 2))
    pvh = np.zeros((128, NPV), np.float32)
    gains = [f(ffn1_g)[0], f(mix_g)[0], f(ffn2_g)[0], f(ffn1_g)[1], f(mix_g)[1], f(ffn2_g)[1], f(final_g)]
    for k, g in enumerate(gains):
        pvh[:, k * 8:(k + 1) * 8] = g.reshape(8, 128).T
    pvh[:, PV_CW:PV_CW + 12] = f(e_conv_w)[0].reshape(3, 4, 128).transpose(2, 0, 1).reshape(128, 12)
    pvh[:, PV_DW:PV_DW + 124] = f(o_dw)[0].reshape(31, 4, 128).transpose(2, 0, 1).reshape(128, 124)
    pvh[:, PV_DWB:PV_DWB + 4] = f(o_dw_b)[0].reshape(4, 128).T
    pvh[:, PV_CNG:PV_CNG + 4] = f(o_cn_g)[0].reshape(4, 128).T
    pvh[:, PV_CNB:PV_CNB + 4] = f(o_cn_b)[0].reshape(4, 128).T
    pbh = np.zeros((128, NPB), np.float32)
    pbh[:, PB_LNG:PB_LNG + 512] = f(o_ln_g)[0][None, :]
    pbh[:, PB_LNB:PB_LNB + 512] = f(o_ln_b)[0][None, :]
    pbh[:, PB_BF:PB_BF + 136] = np.tile(f(e_b_f)[0], 17)[None, :]
    wst = np.ascontiguousarray(f(o_ws)[0].transpose(2, 0, 1))
    bsr = f(o_bs)[0].reshape(1, 1024)
    ck, cvv, clf = f(cache_fox_k)[0], f(cache_fox_v)[0], f(cache_fox_logf)[0]
    ssc, scc = f(state_sconv)[0], f(state_cconv)[0]
    shared = dict(wgu=wgu, wd=wd, ewa=ewa, ewv=ewv, ewo=f(e_w_out)[0], owa=owa, owv=owv, owo=f(o_w_out)[0],
                  pv=pvh, pb=pbh, wst=wst, bsr=bsr)
    in_maps = []
    for b in range(NCORES):
        m = dict(shared)
        m["xT_in"] = np.ascontiguousarray(np.concatenate([x_prompt[b], x_sample[b]], axis=0).T)
        m["ckt"] = np.ascontiguousarray(ck[b].reshape(SEQ, 512).T)
        m["cv"] = np.ascontiguousarray(cvv[b].reshape(16, 128, 4, 128).transpose(2, 1, 0, 3)).reshape(4, 128, 2048)
        m["clf"] = np.ascontiguousarray(clf[b].reshape(16, 128, 8).transpose(1, 0, 2)).reshape(128, 128)
        m["ssc"] = np.ascontiguousarray(ssc[b].T)
        m["scc"] = np.ascontiguousarray(scc[b].T)
        in_maps.append(m)
    return in_maps


def _post(R):
    NCORES = len(R)
    st_ = lambda name, fn: np.stack([fn(np.asarray(R[b][name])) for b in range(NCORES)])
    y_p = st_("yT", lambda a: a[:, :SEQ].T)
    y_s = st_("yT", lambda a: a[:, SEQ:].T)
    pk = st_("kT", lambda a: a[:, :SEQ].T.reshape(SEQ, 8, 64))[None]
    sk = st_("kT", lambda a: a[:, SEQ:].T.reshape(NS, 8, 64))[None]
    pvv = st_("vv", lambda a: a[:SEQ].reshape(SEQ, 8, 64))[None]
    svv = st_("vv", lambda a: a[SEQ:].reshape(NS, 8, 64))[None]
    plf = st_("lf", lambda a: a[:, :128].reshape(128, 16, 8).transpose(1, 0, 2).reshape(SEQ, 8))[None]
    slf = st_("lf", lambda a: a[:16, 128:136])[None]
    psc = st_("scp", lambda a: a.T)[None]
    ssc_o = st_("scs", lambda a: a.T)[None]
    pcc = st_("ccp", lambda a: a.T)[None]
    scc_o = st_("ccs", lambda a: a.T)[None]
    gv = st_("gv", lambda a: a)[None]
    outs = (y_p, y_s, pk, pvv, plf, psc, pcc, sk, svv, slf, ssc_o, scc_o, gv)
    return tuple(np.ascontiguousarray(o, dtype=np.float32) for o in outs)


def kernel(**inputs):
    in_maps = _prep(**inputs)
    if "nc" not in _NC_CACHE:
        _NC_CACHE["nc"] = build_nc()
    res = run_bass_kernel_spmd(_NC_CACHE["nc"], in_maps, core_ids=list(range(NCORES)))
    return _post(res.results)
```

```python
import os
from contextlib import ExitStack
import numpy as np
import concourse.bass as bass
import concourse.mybir as mybir
from concourse.bass_utils import run_bass_kernel_spmd

F32 = mybir.dt.float32
BF16 = mybir.dt.bfloat16
AF = mybir.ActivationFunctionType
ALU = mybir.AluOpType

NCORES = 8
D = 1024
DC = 8
SEQ = 2048
NS = 16
NT = SEQ + NS
DFF = 2816
NF = 22
EPS = 1e-6
TT = [(0, 512), (512, 512), (1024, 512), (1536, 512), (2048, 16)]
TK = [(j * 128, 128) for j in range(16)] + [(2048, 16)]
NEG = -30000.0


class Res:
    __slots__ = ("name", "last_w", "readers", "excl", "key")

    def __init__(self, name, excl=False):
        self.name = name
        self.last_w = None
        self.readers = []
        self.excl = excl
        self.key = None


class DmaKey:
    __slots__ = ("name", "count", "sem", "last_op", "is_w")

    def __init__(self, name, is_w=False):
        self.name = name
        self.count = 0
        self.sem = None
        self.last_op = None
        self.is_w = is_w


class Op:
    __slots__ = ("eng", "fn", "pos", "is_dma", "key", "kcount", "waits", "need_inc", "clock", "inc_count")

    def __init__(self, eng, fn, is_dma, key):
        self.eng = eng
        self.fn = fn
        self.is_dma = is_dma
        self.key = key
        self.kcount = 0
        self.pos = -1
        self.waits = {}
        self.need_inc = False
        self.clock = None
        self.inc_count = 0

    def signal(self):
        if self.is_dma:
            return (self.key, self.kcount)
        return (self.eng, self.pos + 1)


ENGS = ("pe", "act", "dve", "pool", "sp")


class Prog:
    def __init__(self):
        self.streams = {e: [] for e in ENGS}
        self.seen = {e: {} for e in ENGS}
        self.pending = {e: [] for e in ENGS}
        self.keys = []
        self.same_sync = {"act", "dve", "pool", "sp"}
        self.self_waited = {e: 0 for e in ENGS}

    def res(self, name, excl=False):
        return Res(name, excl)

    def kof(self, r):
        if r.key is None:
            r.key = self.key("r" + str(len(self.keys)) + "_" + r.name)
        return r.key

    def key(self, name, is_w=False):
        k = DmaKey(name, is_w)
        self.keys.append(k)
        return k

    def _add(self, eng, fn, reads, writes, is_dma=False, key=None):
        if isinstance(key, Res):
            key = self.kof(key)
        op = Op(eng, fn, is_dma, key)
        stream = self.streams[eng]
        op.pos = len(stream)
        ex = [r for r in reads if r.excl]
        if ex:
            reads = [r for r in reads if not r.excl]
            writes = list(writes) + ex
        deps = []
        for r in reads:
            if r.last_w is not None:
                deps.append(r.last_w)
        for w in writes:
            if w.last_w is not None:
                deps.append(w.last_w)
            deps.extend(w.readers)
        if self.pending[eng]:
            deps.extend(self.pending[eng])
            self.pending[eng] = []
        seen = self.seen[eng]
        for d in deps:
            if d is op:
                continue
            if (not d.is_dma) and d.eng == eng:
                if eng not in self.same_sync:
                    continue
                if self.self_waited[eng] >= d.pos + 1:
                    continue
                self.self_waited[eng] = d.pos + 1
                if op.waits.get(eng, 0) < d.pos + 1:
                    op.waits[eng] = d.pos + 1
                d.need_inc = True
                continue
            name, val = d.signal()
            if d.is_dma:
                val = d.key.count
            if seen.get(name, 0) >= val:
                continue
            if op.waits.get(name, 0) < val:
                op.waits[name] = val
            seen[name] = val
            if not d.is_dma:
                d.need_inc = True
            for n2, v2 in d.clock.items():
                if seen.get(n2, 0) < v2:
                    seen[n2] = v2
        if is_dma:
            key.count += 1
            op.kcount = key.count
            key.last_op = op
            clock = dict(seen)
            clock[key] = op.kcount
        else:
            seen[eng] = op.pos + 1
            clock = dict(seen)
        op.clock = clock
        stream.append(op)
        for r in reads:
            r.readers.append(op)
        for w in writes:
            w.last_w = op
            w.readers = []
        return op

    def op(self, eng, fn, reads=(), writes=()):
        return self._add(eng, fn, reads, writes)

    def dma(self, eng, fn, key, reads=(), writes=()):
        return self._add(eng, fn, reads, writes, is_dma=True, key=key)

    def barrier(self):
        lasts = []
        for e in ("pe", "act", "dve", "sp"):
            for o in reversed(self.streams[e]):
                if not o.is_dma:
                    lasts.append(o)
                    break
        for k in self.keys:
            if k.last_op is not None and not k.is_w:
                lasts.append(k.last_op)
        for e in ("act", "dve", "sp"):
            self.pending[e] = self.pending[e] + list(lasts)

    def emit(self, nc, stack):
        esem = {}
        for e in ENGS:
            esem[e] = stack.enter_context(nc.semaphore("s_" + e))
        for k in self.keys:
            k.sem = stack.enter_context(nc.semaphore("k_" + k.name))
        for e in ENGS:
            c = 0
            for o in self.streams[e]:
                if (not o.is_dma) and o.need_inc:
                    c += 1
                o.inc_count = c
        streams = self.streams
        final = [(k.sem, 16 * k.count) for k in self.keys if k.count]

        def run(e, eng):
            for o in streams[e]:
                for name, val in o.waits.items():
                    if isinstance(name, DmaKey):
                        eng.wait_ge(name.sem, 16 * val)
                    else:
                        eng.wait_ge(esem[name], streams[name][val - 1].inc_count)
                ins = o.fn(eng)
                if o.is_dma:
                    ins.then_inc(o.key.sem, 16)
                elif o.need_inc:
                    ins.then_inc(esem[e], 1)
            if e == "sp":
                for s, v in final:
                    eng.wait_ge(s, v)

        block = stack.enter_context(nc.Block())

        @block.tensor
        def _(eng):
            run("pe", eng)

        @block.scalar
        def _(eng):
            run("act", eng)

        @block.vector
        def _(eng):
            run("dve", eng)

        @block.gpsimd
        def _(eng):
            run("pool", eng)

        @block.sync
        def _(eng):
            run("sp", eng)


class WItem:
    __slots__ = ("pool", "loads", "slot", "issued", "done", "ap", "res")

    def __init__(self, pool, loads):
        self.pool = pool
        self.loads = loads
        self.slot = -1
        self.issued = False
        self.done = False


class WQ:
    def __init__(self, p, pools):
        self.p = p
        self.pools = pools
        self.items = []
        self.next_issue = 0
        self.next_use = 0
        self.occ = {n: [None] * len(v) for n, v in pools.items()}
        self.cnt = {n: 0 for n in pools}

    def add(self, pool, loads):
        self.items.append(WItem(pool, loads))

    def pump(self):
        while self.next_issue < len(self.items):
            it = self.items[self.next_issue]
            ring = self.pools[it.pool]
            s = self.cnt[it.pool] % len(ring)
            prev = self.occ[it.pool][s]
            if prev is not None and not prev.done:
                return
            ap, res, key = ring[s]
            it.slot, it.ap, it.res = s, ap, res
            for ld in it.loads:
                self.p.dma("pool", (lambda e, ld=ld, ap=ap: ld(e, ap)), key, writes=[res])
            it.issued = True
            self.occ[it.pool][s] = it
            self.cnt[it.pool] += 1
            self.next_issue += 1

    def next(self):
        self.pump()
        it = self.items[self.next_use]
        assert it.issued, "weight item not issuable (ring too small for outstanding items)"
        self.next_use += 1
        return it

    def release(self, it):
        it.done = True
        self.pump()


E_ORDER = [0, 8, 4, 1, 9, 5, 2, 10, 6, 3, 11, 7, 12, 16, 13, 17, 14, 18, 15, 19]
O_ORDER = [0, 1, 2, 3, 8, 12, 9, 13, 10, 14, 11, 15]
PV_G = 0
PV_CW = 56
PV_DW = 68
PV_DWB = 192
PV_CNG = 196
PV_CNB = 200
NPV = 204
PB_LNG = 0
PB_LNB = 512
PB_BF = 1024
NPB = 1024 + 17 * 8


def build_nc(stages=None):
    if stages is None:
        stages = os.environ.get("MK_STAGES", "f0,em,f1,f2,om,f3,fn").split(",")
    nc = bass.Bass("TRN2", target_bir_lowering=False)

    def din(name, shape):
        return nc.dram_tensor(name, list(shape), F32, kind="ExternalInput").ap()

    def dout(name, shape):
        return nc.dram_tensor(name, list(shape), F32, kind="ExternalOutput").ap()

    x_in = din("xT_in", [D, NT])
    wgu_in = din("wgu", [4, NF, 128, 2048])
    wd_in = din("wd", [4, DFF, D])
    ewa_in = din("ewa", [10, 128, 2048])
    ewv_in = din("ewv", [128, 8, 520])
    ewo_in = din("ewo", [D, D])
    owa_in = din("owa", [6, 128, 2048])
    owv_in = din("owv", [128, 8, 512])
    owo_in = din("owo", [D, D])
    pv_in = din("pv", [128, NPV])
    pb_in = din("pb", [128, NPB])
    wst_in = din("wst", [128, 8, 128])
    bs_in = din("bsr", [1, 1024])
    ckt_in = din("ckt", [512, SEQ])
    cv_in = din("cv", [4, 128, 2048])
    clf_in = din("clf", [128, 128])
    ssc_in = din("ssc", [512, 2])
    scc_in = din("scc", [512, 30])

    yT_o = dout("yT", [D, NT])
    kT_o = dout("kT", [512, NT])
    v_o = dout("vv", [NT, 512])
    lf_o = dout("lf", [128, 136])
    scp_o = dout("scp", [512, 2])
    scs_o = dout("scs", [512, 2])
    ccp_o = dout("ccp", [512, 30])
    ccs_o = dout("ccs", [512, 30])
    gv_o = dout("gv", [NS, 512])

    p = Prog()
    MARKS.clear()
    st = ExitStack()
    with st:
        def sb(name, shape, dt):
            return st.enter_context(nc.sbuf_tensor(name, shape, dt))

        xT = sb("xT", [128, DC, NT], F32)
        r_xT = [[p.res(f"xT{c}_{t}") for t in range(5)] for c in range(DC)]
        poolA_t = sb("poolA", [128, 4, 2048], BF16)
        poolB_t = sb("poolB", [128, 12, 1024], BF16)
        pv = sb("pv_sb", [128, NPV], F32)
        pbc = sb("pbc", [128, NPB], F32)
        ident_f = sb("ident_f", [128, 128], F32)
        ident_b = sb("ident_b", [128, 128], BF16)
        ones_b = sb("ones_b", [128, 128], BF16)
        ones_f = sb("ones_f", [128, 128], F32)
        tri_f = sb("tri_f", [128, 128], F32)
        esel = sb("esel", [128, 2, 128], F32)
        maskneg = sb("maskneg", [128, 128], BF16)
        wstm = sb("wstm", [128, 8, 128], BF16)
        ARENA_W = 23700
        arena = sb("arena", [128, ARENA_W], F32)
        r_const = p.res("const")

        PS = [st.enter_context(nc.psum_tensor(f"ps{i}", [128, 512], F32)) for i in range(8)]
        r_ps = [p.res(f"ps{i}", excl=True) for i in range(8)]


        class Carver:
            def __init__(self):
                self.off = 0

            def f32(self, n):
                a = arena[:, self.off:self.off + n]
                self.off += n
                assert self.off <= ARENA_W, self.off
                return a

            def bf16(self, n):
                w = (n + 1) // 2
                a = arena[:, self.off:self.off + w].bitcast(BF16)
                self.off += w
                assert self.off <= ARENA_W, self.off
                return a

        poolA = [(poolA_t[:, s, :], p.res(f"A{s}"), p.key(f"A{s}", True)) for s in range(4)]
        poolB = [(poolB_t[:, s, :], p.res(f"B{s}"), p.key(f"B{s}", True)) for s in range(12)]
        wq = WQ(p, {"A": poolA, "B": poolB})

        def ld_full(src):
            return lambda e, ap: e.dma_start(out=ap, in_=src)

        def ld_part(src, n):
            return lambda e, ap: e.dma_start(out=ap[:, 0:n], in_=src)

        GROUPS = [(0, 4), (4, 4), (8, 4), (12, 4), (16, 6)]

        def add_even_w():
            for s in range(6):
                wq.add("A", [ld_full(ewa_in[s])])
            for fc in range(4):
                wq.add("B", [ld_full(ewo_in[fc * 128:(fc + 1) * 128, :])])
            for c in range(8):
                wq.add("B", [ld_part(ewv_in[:, c, :], 520)])
            for pr in range(4):
                wq.add("A", [ld_full(ewa_in[6 + pr])])
                wq.add("B", [ld_full(ewo_in[(4 + pr) * 128:(5 + pr) * 128, :])])

        def add_odd_w():
            for c in range(8):
                wq.add("B", [ld_part(owv_in[:, c, :], 512)])
            for s in range(2):
                wq.add("A", [ld_full(owa_in[s])])
            for fc in range(4):
                wq.add("B", [ld_full(owo_in[fc * 128:(fc + 1) * 128, :])])
            for s in range(2, 6):
                wq.add("A", [ld_full(owa_in[s])])
            for fc in range(4, 8):
                wq.add("B", [ld_full(owo_in[fc * 128:(fc + 1) * 128, :])])


        r_pv = p.res("pv")
        p.dma("sp", lambda e: e.dma_start(out=pv[:], in_=pv_in), r_pv, writes=[r_pv])
        p.dma("sp", lambda e: e.dma_start(out=pbc[:], in_=pb_in), r_pv, writes=[r_pv])
        wst_f = arena[:, 0:1024]
        r_wstf = p.res("wstf")
        p.dma("sp", lambda e: e.dma_start(out=wst_f.rearrange("p (g t) -> p g t", g=8), in_=wst_in), r_wstf,
              writes=[r_wstf])

        def cset(fn):
            p.op("pool", fn, reads=[r_const], writes=[r_const])

        cset(lambda e: e.memset(ident_f[:], 0.0))
        cset(lambda e: e.affine_select(out=ident_f[:], in_=ident_f[:], compare_op=ALU.not_equal, fill=1.0,
                                       base=0, pattern=[[-1, 128]], channel_multiplier=1))
        cset(lambda e: e.tensor_copy(out=ident_b[:], in_=ident_f[:]))
        cset(lambda e: e.memset(ones_b[:], 1.0))
        cset(lambda e: e.memset(ones_f[:], 1.0))
        cset(lambda e: e.affine_select(out=tri_f[:], in_=ones_f[:], compare_op=ALU.is_ge, fill=0.0,
                                       base=0, pattern=[[1, 128]], channel_multiplier=-1))
        cset(lambda e: e.memset(maskneg[:], 0.0))
        cset(lambda e: e.affine_select(out=maskneg[:], in_=maskneg[:], compare_op=ALU.is_ge, fill=NEG,
                                       base=0, pattern=[[1, 128]], channel_multiplier=-1))
        cset(lambda e: e.memset(esel[:], 0.0))
        cset(lambda e: e.affine_select(out=esel[:, 0, :], in_=esel[:, 0, :], compare_op=ALU.not_equal, fill=1.0,
                                       base=-127, pattern=[[0, 128]], channel_multiplier=1))
        cset(lambda e: e.affine_select(out=esel[:, 1, :], in_=esel[:, 1, :], compare_op=ALU.not_equal, fill=1.0,
                                       base=-15, pattern=[[0, 128]], channel_multiplier=1))
        for g in range(8):
            p.op("dve", lambda e, g=g: e.tensor_tensor(out=wstm[:, g, :], in0=wst_f[:, g * 128:(g + 1) * 128],
                                                       in1=tri_f[:], op=ALU.mult),
                 reads=[r_wstf, r_const], writes=[r_pv])
        p.barrier()
        for t, (t0, n) in enumerate(TT):
            for c in range(DC):
                p.dma("sp", lambda e, c=c, t0=t0, n=n: e.dma_start(out=xT[:, c, t0:t0 + n], in_=x_in[c * 128:(c + 1) * 128, t0:t0 + n]),
                      r_xT[c][t], writes=[r_xT[c][t]])

        def mm(out, lhsT, rhs, start, stop, reads, writes):
            p.op("pe", lambda e: e.matmul(out, lhsT=lhsT, rhs=rhs, start=start, stop=stop, skip_group_check=True),
                 reads=reads, writes=writes)

        XN_W = DC * NT // 2
        xn = arena[:, 0:XN_W].bitcast(BF16).rearrange("p (c n) -> p c n", c=DC)
        r_xn = [[p.res(f"xn{c}_{t}") for t in range(5)] for c in range(DC)]
        rstd2 = [arena[:, XN_W:XN_W + 512]] * 2
        r_rstd2 = [p.res("rstd")] * 2
        BASE = XN_W + 512
        yst = [arena[:, BASE + 6192 + 512 * b_:BASE + 6192 + 512 * (b_ + 1)] for b_ in range(4)]
        r_yst = [p.res(f"yst{b_}") for b_ in range(4)]
        ycnt_f = [0]
        SQ8 = [None]

        def set_sq8(ap2d):
            SQ8[0] = (ap2d.rearrange("p (c n) -> p c n", c=DC), [p.res(f"sq8_{c}") for c in range(DC)])

        def norm_tile(t, gidx, ssb=6, final=False, defer=False):
            t0, n = TT[t]
            ss = PS[ssb]
            rstd, r_rstd = rstd2[t % 2], r_rstd2[t % 2]
            sq8, r_sq8 = SQ8[0]
            for c in range(DC):
                p.op("act", lambda e, c=c: e.activation(out=sq8[:, c, :n], in_=xT[:, c, t0:t0 + n], func=AF.Square),
                     reads=[r_xT[c][t]], writes=[r_sq8[c]])
            for c in range(DC):
                mm(ss[:, :n], ones_b[:], sq8[:, c, :n], c == 0, c == DC - 1, [r_sq8[c], r_const], [r_ps[ssb]])
            p.op("act", lambda e: e.activation(out=rstd[:, :n], in_=ss[:, :n], func=AF.Ln, scale=1.0 / D, bias=EPS),
                 reads=[r_ps[ssb]], writes=[r_rstd])
            p.op("act", lambda e: e.activation(out=rstd[:, :n], in_=rstd[:, :n], func=AF.Exp, scale=-0.5),
                 reads=[r_rstd], writes=[r_rstd])
            def out_job(c):
                gs = pv[:, gidx * 8 + c:gidx * 8 + c + 1]
                if not final:
                    p.op("dve", lambda e: e.scalar_tensor_tensor(
                        out=xn[:, c, t0:t0 + n], in0=xT[:, c, t0:t0 + n], scalar=gs, in1=rstd[:, :n], op0=ALU.mult, op1=ALU.mult),
                        reads=[r_xT[c][t], r_rstd, r_pv], writes=[r_xn[c][t]])
                else:
                    yb_ = ycnt_f[0] % 4
                    ycnt_f[0] += 1
                    p.op("dve", lambda e: e.scalar_tensor_tensor(
                        out=yst[yb_][:, :n], in0=xT[:, c, t0:t0 + n], scalar=gs, in1=rstd[:, :n], op0=ALU.mult, op1=ALU.mult),
                        reads=[r_xT[c][t], r_rstd, r_pv], writes=[r_yst[yb_]])
                    p.dma("sp", lambda e: e.dma_start(out=yT_o[c * 128:(c + 1) * 128, t0:t0 + n], in_=yst[yb_][:, :n]),
                          r_yst[yb_], reads=[r_yst[yb_]])

            jobs = [(lambda c=c: out_job(c)) for c in range(DC)]
            if defer:
                return jobs
            for j_ in jobs:
                j_()
            return []

        def tail_tiles(tile_fn, nxt):
            gidx, final = nxt
            pend = []

            def hook():
                if pend:
                    pend.pop(0)()

            for t in range(5):
                tile_fn(t, hook)
                while pend:
                    pend.pop(0)()
                if t >= 1:
                    pend.extend(norm_tile(t - 1, gidx, final=final, defer=True))
            while pend:
                pend.pop(0)()
            norm_tile(4, gidx, final=final)

        def ffn_jobs():
            jobs = []
            for gi, (f0, G) in enumerate(GROUPS):
                for fl in range(G):
                    if gi > 0 and fl == 0:
                        continue
                    jobs.append(("gu", gi, fl))
                if gi + 1 < len(GROUPS):
                    jobs.append(("gu", gi + 1, 0))
                jobs.append(("down", gi, None))
            return jobs

        def add_ffn_w(i):
            for kind, gi, fl in ffn_jobs():
                f0, G = GROUPS[gi]
                if kind == "gu":
                    wq.add("A", [ld_full(wgu_in[i, f0 + fl])])
                else:
                    for f in range(f0, f0 + G):
                        wq.add("B", [ld_full(wd_in[i, f * 128:(f + 1) * 128, :])])

        def ffn(i, nxt):
            cv = Carver()
            cv.off = BASE
            hT = [cv.bf16(6 * NT).rearrange("p (c n) -> p c n", c=6), cv.bf16(4 * NT).rearrange("p (c n) -> p c n", c=4)]
            r_h = [p.res(f"h{b}") for b in range(2)]
            sg = [cv.bf16(512) for _ in range(2)]
            r_sg = [p.res(f"sg{b}") for b in range(2)]
            ffn_sq8 = cv.bf16(DC * 512)
            cnt = [0]
            ycnt = [0]

            def gu(gi, fl):
                hb = gi % 2
                it = wq.next()
                w = it.ap.rearrange("p (m c j) -> p m c j", m=2, c=DC)
                for t, (t0, n) in enumerate(TT):
                    b = cnt[0] % 2
                    cnt[0] += 1
                    gps, ups = PS[b], PS[2 + b]
                    for c in range(DC):
                        mm(gps[:, :n], w[:, 0, c, :], xn[:, c, t0:t0 + n], c == 0, c == DC - 1, [it.res, r_xn[c][t]], [r_ps[b]])
                    for c in range(DC):
                        mm(ups[:, :n], w[:, 1, c, :], xn[:, c, t0:t0 + n], c == 0, c == DC - 1, [it.res, r_xn[c][t]], [r_ps[2 + b]])
                    p.op("act", lambda e, b=b, n=n, gps=gps: e.activation(out=sg[b][:, :n], in_=gps[:, :n], func=AF.Silu),
                         reads=[r_ps[b]], writes=[r_sg[b]])
                    p.op("dve", lambda e, b=b, n=n, t0=t0, ups=ups: e.tensor_tensor(
                        out=hT[hb][:, fl, t0:t0 + n], in0=ups[:, :n], in1=sg[b][:, :n], op=ALU.mult),
                        reads=[r_ps[2 + b], r_sg[b]], writes=[r_h[hb]])
                wq.release(it)

            def down_ct(wds, gi, c, t):
                hb = gi % 2
                G = GROUPS[gi][1]
                t0, n = TT[t]
                b = (4, 5, 7)[ycnt[0] % 3]
                ycnt[0] += 1
                yps = PS[b]
                for fl in range(G):
                    mm(yps[:, :n], wds[fl].ap[:, c * 128:(c + 1) * 128], hT[hb][:, fl, t0:t0 + n],
                       fl == 0, fl == G - 1, [wds[fl].res, r_h[hb]], [r_ps[b]])
                p.op("dve", lambda e: e.scalar_tensor_tensor(
                    out=xT[:, c, t0:t0 + n], in0=yps[:, :n], scalar=0.5, in1=xT[:, c, t0:t0 + n], op0=ALU.mult, op1=ALU.add),
                    reads=[r_ps[b], r_xT[c][t]], writes=[r_xT[c][t]])

            for kind, gi, fl in ffn_jobs():
                if kind == "gu":
                    gu(gi, fl)
                    continue
                G = GROUPS[gi][1]
                wds = [wq.next() for _ in range(G)]
                if gi < len(GROUPS) - 1:
                    for c in range(DC):
                        for t in range(5):
                            down_ct(wds, gi, c, t)
                else:
                    def tile_fn(t, hook, wds=wds, gi=gi):
                        for c in range(DC):
                            down_ct(wds, gi, c, t)
                            hook()
                    set_sq8(ffn_sq8)
                    tail_tiles(tile_fn, nxt)
                for it in wds:
                    wq.release(it)
            p.barrier()

        def project(its, srcs, r_srcs, bank0=4, nxt=None):
            cntp = [0]
            K = len(its)

            def one(c, t):
                t0, n = TT[t]
                b = ((6, 7, 5) if bank0 == 6 else (4, 5, 7))[cntp[0] % 3]
                cntp[0] += 1
                yps = PS[b]
                for k in range(K):
                    mm(yps[:, :n], its[k].ap[:, c * 128:(c + 1) * 128], srcs[k][:, t0:t0 + n], k == 0, k == K - 1,
                       [its[k].res] + r_srcs, [r_ps[b]])
                p.op("dve", lambda e: e.tensor_tensor(out=xT[:, c, t0:t0 + n], in0=yps[:, :n], in1=xT[:, c, t0:t0 + n], op=ALU.add),
                     reads=[r_ps[b], r_xT[c][t]], writes=[r_xT[c][t]])

            if nxt is None:
                for c in range(DC):
                    for t in range(5):
                        one(c, t)
            else:
                def tile_fn(t, hook):
                    for c in range(DC):
                        one(c, t)
                        hook()
                tail_tiles(tile_fn, nxt)

        def even_mixer(nxt):
            cv = Carver()
            base_off = BASE
            cv.off = base_off

            yaT = cv.bf16(4 * NT).rearrange("p (c n) -> p c n", c=4)
            r_ya = p.res("ya")
            uT = [cv.f32(2 + SEQ + 2 + NS) for _ in range(2)]
            r_u = [p.res(f"u{b}") for b in range(2)]
            xa_s = [cv.f32(512) for _ in range(2)]
            r_xa = [p.res(f"xa{b}") for b in range(2)]
            acc = [cv.f32(512) for _ in range(2)]
            r_acc = [p.res(f"acc{b}") for b in range(2)]
            cnt = 0
            fetched = {}

            def chunkw(flat):
                s = flat // 2
                if s not in fetched:
                    fetched[s] = wq.next()
                it = fetched[s]
                return it, it.ap.rearrange("p (m c j) -> p m c j", m=2, c=DC)[:, flat % 2]

            def rel_upto(flat):
                for s in list(fetched):
                    if 2 * s + 1 <= flat and not fetched[s].done:
                        wq.release(fetched[s])

            for cc in range(4):
                ub = uT[cc % 2]
                r_ub = r_u[cc % 2]
                p.op("dve", lambda e, ub=ub: e.memset(ub[:, 0:2], 0.0), writes=[r_ub])
                p.dma("sp", lambda e, ub=ub, cc=cc: e.dma_start(out=ub[:, 2 + SEQ:4 + SEQ], in_=ssc_in[cc * 128:(cc + 1) * 128, :]),
                      r_ub, writes=[r_ub])
                it_xa, w_xa = chunkw(3 * cc)
                it_gc, w_gc = chunkw(3 * cc + 1)
                it_gb, w_gb = chunkw(3 * cc + 2)
                for t, (t0, n) in enumerate(TT):
                    b = cnt % 2
                    cnt += 1
                    pxa, pgc, pgb = PS[b], PS[2 + b], PS[4 + b]
                    for c in range(DC):
                        mm(pxa[:, :n], w_xa[:, c, :], xn[:, c, t0:t0 + n], c == 0, c == DC - 1, [it_xa.res, r_xn[c][t]], [r_ps[b]])
                    for c in range(DC):
                        mm(pgc[:, :n], w_gc[:, c, :], xn[:, c, t0:t0 + n], c == 0, c == DC - 1, [it_gc.res, r_xn[c][t]], [r_ps[2 + b]])
                    for c in range(DC):
                        mm(pgb[:, :n], w_gb[:, c, :], xn[:, c, t0:t0 + n], c == 0, c == DC - 1, [it_gb.res, r_xn[c][t]], [r_ps[4 + b]])
                    u0 = 2 + t0 if t < 4 else 4 + SEQ
                    p.op("act", lambda e, b=b, n=n, pxa=pxa: e.activation(out=xa_s[b][:, :n], in_=pxa[:, :n], func=AF.Copy),
                         reads=[r_ps[b]], writes=[r_xa[b]])
                    p.op("dve", lambda e, b=b, n=n, u0=u0, ub=ub, pgc=pgc: e.tensor_tensor(
                        out=ub[:, u0:u0 + n], in0=pgc[:, :n], in1=xa_s[b][:, :n], op=ALU.mult),
                        reads=[r_ps[2 + b], r_xa[b]], writes=[r_ub])
                    cw = [pv[:, PV_CW + j * 4 + cc:PV_CW + j * 4 + cc + 1] for j in range(3)]
                    p.op("dve", lambda e, b=b, n=n, u0=u0, ub=ub, cw=cw: e.tensor_scalar(
                        out=acc[b][:, :n], in0=ub[:, u0 - 2:u0 - 2 + n], scalar1=cw[0], scalar2=None, op0=ALU.mult),
                        reads=[r_ub, r_pv], writes=[r_acc[b]])
                    for j in (1, 2):
                        p.op("dve", lambda e, b=b, n=n, u0=u0, ub=ub, cw=cw, j=j: e.scalar_tensor_tensor(
                            out=acc[b][:, :n], in0=ub[:, u0 - 2 + j:u0 - 2 + j + n], scalar=cw[j], in1=acc[b][:, :n],
                            op0=ALU.mult, op1=ALU.add),
                            reads=[r_ub, r_pv, r_acc[b]], writes=[r_acc[b]])
                    p.op("dve", lambda e, b=b, n=n, t0=t0, cc=cc, pgb=pgb: e.tensor_tensor(
                        out=yaT[:, cc, t0:t0 + n], in0=pgb[:, :n], in1=acc[b][:, :n], op=ALU.mult),
                        reads=[r_ps[4 + b], r_acc[b]], writes=[r_ya])
                rel_upto(3 * cc + 2)
                p.dma("sp", lambda e, ub=ub, cc=cc: e.dma_start(out=scp_o[cc * 128:(cc + 1) * 128, :], in_=ub[:, SEQ:SEQ + 2]),
                      r_ub, reads=[r_ub])
                p.dma("sp", lambda e, ub=ub, cc=cc: e.dma_start(out=scs_o[cc * 128:(cc + 1) * 128, :],
                                                               in_=ub[:, 2 + SEQ + NS:4 + SEQ + NS]),
                      r_ub, reads=[r_ub])
            for s in list(fetched):
                if not fetched[s].done:
                    wq.release(fetched[s])
            wo = [wq.next() for _ in range(4)]
            project(wo, [yaT[:, k, :] for k in range(4)], [r_ya], bank0=6)
            for it in wo:
                wq.release(it)
            p.barrier()

            mark(p, 'em_V')
            EM = int(os.environ.get("MK_EM", "9"))
            if EM <= 1:
                return
            cv.off = base_off
            Vb = cv.bf16(17 * 512).rearrange("p (j n) -> p j n", j=17)
            r_V = p.res("V")
            vst = [cv.f32(512) for _ in range(2)]
            r_vst = [p.res(f"vst{b}") for b in range(2)]
            lf = cv.f32(17 * 8)
            r_lf = p.res("lf")
            plf = cv.f32(16 * 8)
            r_plf = p.res("plf")
            cP = cv.f32(16 * 8)
            cS = cv.f32(17 * 8)
            r_c = p.res("c")
            biasP = cv.f32(40 * 8)
            biasS = cv.f32(17 * 8)
            r_bias = p.res("bias")
            off_ck = cv.off
            cKT = cv.bf16(SEQ)
            cVb = cv.bf16(16 * 128).rearrange("p (j n) -> p j n", j=16)
            em_sq8 = arena[:, off_ck:off_ck + DC * 256].bitcast(BF16)
            wv = [wq.next() for _ in range(8)]
            p.dma("sp", lambda e: e.dma_start(out=plf, in_=clf_in), r_plf, writes=[r_plf])
            flps = PS[7]
            vst_x = [arena[:, off_ck + 512 * b_:off_ck + 512 * (b_ + 1)] for b_ in range(2)]
            r_vst_x = [p.res(f"vstx{b_}") for b_ in range(2)]
            vst4 = [vst[0], vst[1], vst_x[0], vst_x[1]]
            r_vst4 = [r_vst[0], r_vst[1], r_vst_x[0], r_vst_x[1]]
            for j, (t0, n) in enumerate(TK):
                b = j % 2
                vps = PS[b]
                for c in range(DC):
                    mm(vps[:n, :], xn[:, c, t0:t0 + n], wv[c].ap[:, 0:512], c == 0, c == DC - 1,
                       [wv[c].res, r_xn[c][min(j // 4, 4)]], [r_ps[b]])
                for c in range(DC):
                    mm(flps[:n, j * 8:(j + 1) * 8], xn[:, c, t0:t0 + n], wv[c].ap[:, 512:520], c == 0, c == DC - 1,
                       [wv[c].res, r_xn[c][min(j // 4, 4)]], [r_ps[7]])
                sb4 = j % 4
                p.op("act", lambda e, sb4=sb4, n=n, vps=vps: e.activation(out=vst4[sb4][:n, :], in_=vps[:n, :], func=AF.Copy),
                     reads=[r_ps[b]], writes=[r_vst4[sb4]])
                p.op("dve", lambda e, b=b, n=n, j=j, vps=vps: e.tensor_copy(out=Vb[:n, j, :], in_=vps[:n, :]),
                     reads=[r_ps[b]], writes=[r_V])
                p.dma("sp", lambda e, sb4=sb4, n=n, t0=t0: e.dma_start(out=v_o[t0:t0 + n, :], in_=vst4[sb4][:n, :]), r_vst4[sb4],
                      reads=[r_vst4[sb4]])
            for it in wv:
                wq.release(it)
            if EM <= 2:
                p.barrier()
                return
            p.op("dve", lambda e: e.memset(lf[:, 128:136], 0.0), writes=[r_lf])
            p.op("dve", lambda e: e.tensor_tensor(out=lf[:, 0:128], in0=flps[:, 0:128], in1=pbc[:, PB_BF:PB_BF + 128], op=ALU.add),
                 reads=[r_ps[7], r_pv], writes=[r_lf])
            p.op("dve", lambda e: e.tensor_tensor(out=lf[0:16, 128:136], in0=flps[0:16, 128:136],
                                                  in1=pbc[0:16, PB_BF + 128:PB_BF + 136], op=ALU.add),
                 reads=[r_ps[7], r_pv], writes=[r_lf])
            p.op("act", lambda e: e.activation(out=lf[:], in_=lf[:], func=AF.Exp, scale=-1.0), reads=[r_lf], writes=[r_lf])
            p.op("act", lambda e: e.activation(out=lf[:], in_=lf[:], func=AF.Ln, bias=1.0), reads=[r_lf], writes=[r_lf])
            p.op("dve", lambda e: e.tensor_scalar(out=lf[:], in0=lf[:], scalar1=-1.0, scalar2=None, op0=ALU.mult),
                 reads=[r_lf], writes=[r_lf])
            p.dma("sp", lambda e: e.dma_start(out=lf_o, in_=lf), r_lf, reads=[r_lf])
            if EM <= 3:
                p.barrier()
                return
            def emit_cumsum():
                mark(p, 'em_cumsum')
                Pp = cv.f32(16 * 8)
                Ps = cv.f32(17 * 8)
                r_pp = p.res("pp")
                p.op("dve", lambda e: e.memset(Pp[:, 0:8], 0.0), writes=[r_pp])
                p.op("dve", lambda e: e.memset(Ps[:, 0:8], 0.0), writes=[r_pp])
                for j in range(15):
                    p.op("dve", lambda e, j=j: e.tensor_tensor(out=Pp[:, (j + 1) * 8:(j + 2) * 8], in0=Pp[:, j * 8:(j + 1) * 8],
                                                               in1=lf[:, j * 8:(j + 1) * 8], op=ALU.add), reads=[r_lf, r_pp], writes=[r_pp])
                for j in range(16):
                    p.op("dve", lambda e, j=j: e.tensor_tensor(out=Ps[:, (j + 1) * 8:(j + 2) * 8], in0=Ps[:, j * 8:(j + 1) * 8],
                                                               in1=plf[:, j * 8:(j + 1) * 8], op=ALU.add), reads=[r_plf, r_pp], writes=[r_pp])
                cps = PS[6]
                for j in range(16):
                    mm(cps[:, j * 8:(j + 1) * 8], tri_f[:], lf[:, j * 8:(j + 1) * 8], True, j == 0, [r_lf, r_const], [r_ps[6]])
                    if j > 0:
                        mm(cps[:, j * 8:(j + 1) * 8], ones_f[:], Pp[:, j * 8:(j + 1) * 8], False, True, [r_pp, r_const], [r_ps[6]])
                p.op("dve", lambda e: e.tensor_copy(out=cP[:], in_=cps[:, 0:128]), reads=[r_ps[6]], writes=[r_c])
                cps2 = PS[5]
                for j in range(16):
                    mm(cps2[:, j * 8:(j + 1) * 8], tri_f[:], plf[:, j * 8:(j + 1) * 8], True, j == 0, [r_plf, r_const], [r_ps[5]])
                    if j > 0:
                        mm(cps2[:, j * 8:(j + 1) * 8], ones_f[:], Ps[:, j * 8:(j + 1) * 8], False, True, [r_pp, r_const], [r_ps[5]])
                mm(cps2[:16, 128:136], tri_f[0:16, 0:16], lf[0:16, 128:136], True, False, [r_lf, r_const], [r_ps[5]])
                mm(cps2[:16, 128:136], ones_f[:, 0:16], Ps[:, 128:136], False, True, [r_pp, r_const], [r_ps[5]])
                p.op("dve", lambda e: e.tensor_copy(out=cS[:, 0:128], in_=cps2[:, 0:128]), reads=[r_ps[5]], writes=[r_c])
                p.op("dve", lambda e: e.tensor_copy(out=cS[0:16, 128:136], in_=cps2[0:16, 128:136]), reads=[r_ps[5]], writes=[r_c])
                eps_ = PS[4]
                for Q in range(4):
                    j = 4 * Q + 3
                    mm(eps_[:, Q * 8:(Q + 1) * 8], esel[:, 0, :], cP[:, j * 8:(j + 1) * 8], True, True, [r_c, r_const], [r_ps[4]])
                mm(eps_[:, 32:40], esel[0:16, 1, :], cS[0:16, 128:136], True, True, [r_c, r_const], [r_ps[4]])
                bidx = {}
                nb = 0
                for Q in range(4):
                    for j in range(4 * Q + 4):
                        bidx[(Q, j)] = nb
                        p.op("dve", lambda e, Q=Q, j=j, nb=nb: e.tensor_tensor(
                            out=biasP[:, nb * 8:(nb + 1) * 8], in0=eps_[:, Q * 8:(Q + 1) * 8], in1=cP[:, j * 8:(j + 1) * 8],
                            op=ALU.subtract), reads=[r_ps[4], r_c], writes=[r_bias])
                        nb += 1
                for j in range(17):
                    nj = 128 if j < 16 else 16
                    p.op("dve", lambda e, j=j, nj=nj: e.tensor_tensor(
                        out=biasS[:nj, j * 8:(j + 1) * 8], in0=eps_[:nj, 32:40], in1=cS[:nj, j * 8:(j + 1) * 8],
                        op=ALU.subtract), reads=[r_ps[4], r_c], writes=[r_bias])
                return bidx

            if EM <= 4:
                p.barrier()
                return
            mark(p, 'em_attn')
            Qz = [cv.bf16(NT), cv.bf16(NT)]
            KT = cv.bf16(NT)
            r_qk = p.res("qk")
            p.op("dve", lambda e: e.memset(Qz[0][64:128, :], 0.0), writes=[r_qk])
            p.op("dve", lambda e: e.memset(Qz[1][0:64, :], 0.0), writes=[r_qk])
            r_ck = p.res("ck")
            k_ck = r_ck
            ybT = cv.bf16(NT)
            r_ybt = [p.res(f"yb{t}") for t in range(5)]
            kst, r_kst = vst, r_vst
            pT = [cv.bf16(512) for _ in range(3)]
            r_pT = [p.res(f"pT{b}") for b in range(3)]
            rl, r_rl = vst[1], r_vst[1]
            pcnt = [0]
            scnt = [0]

            pending_cb = []

            def attn_run(groups, tile_done, flush):
                steps = []
                for gi, (hh, q0, nq, ktiles) in enumerate(groups):
                    for idx, kt in enumerate(ktiles):
                        steps.append((gi, idx, len(ktiles), hh, q0, nq, kt))

                def emit_qk(si):
                    gi, idx, nkt, hh, q0, nq, (kt_ap, v_ap, b_ap, nk, n0, rr) = steps[si]
                    rows = slice(hh * 64, hh * 64 + 64)
                    sb_ = (2, 3, 1)[si % 3]
                    sps = PS[sb_]
                    diag = n0 is not None
                    n0_ = n0 if diag else 0
                    N = nq - n0_
                    dn = min(128, N) if diag else 0
                    QH = Qz[hh]
                    if diag:
                        mm(sps[:nk, 0:dn], ident_b[:nk, :nk], maskneg[:nk, 0:dn], True, False, [r_const], [r_ps[sb_]])
                        mm(sps[:nk, 0:dn], kt_ap, QH[:, q0 + n0_:q0 + n0_ + dn], False, True, [r_qk] + rr, [r_ps[sb_]])
                        if N > dn:
                            mm(sps[:nk, dn:N], kt_ap, QH[:, q0 + n0_ + dn:q0 + nq], True, True, [r_qk] + rr, [r_ps[sb_]])
                    else:
                        mm(sps[:nk, 0:N], kt_ap, QH[:, q0:q0 + nq], True, True, [r_qk] + rr, [r_ps[sb_]])

                def emit_rest(si):
                    gi, idx, nkt, hh, q0, nq, (kt_ap, v_ap, b_ap, nk, n0, rr) = steps[si]
                    rows = slice(hh * 64, hh * 64 + 64)
                    sb_ = (2, 3, 1)[si % 3]
                    sps = PS[sb_]
                    obank = 4 + 2 * (gi % 2)
                    ops_, lps_ = PS[obank], PS[obank + 1]
                    n0_ = n0 if n0 is not None else 0
                    N = nq - n0_
                    pb_ = si % 3
                    p.op("act", lambda e: e.activation(out=pT[pb_][:nk, 0:N], in_=sps[:nk, 0:N], func=AF.Exp, bias=b_ap, scale=0.125),
                         reads=[r_ps[sb_], r_bias], writes=[r_pT[pb_]])
                    mm(ops_[:, n0_:nq], v_ap, pT[pb_][:nk, 0:N], idx == 0, idx == nkt - 1, [r_pT[pb_], r_V] + rr, [r_ps[obank]])
                    mm(lps_[:, n0_:nq], ones_b[:nk, :], pT[pb_][:nk, 0:N], idx == 0, idx == nkt - 1, [r_pT[pb_], r_const],
                       [r_ps[obank + 1]])
                    if idx == nkt - 1:
                        p.op("dve", lambda e: e.reciprocal(out=rl[rows, 0:nq], in_=lps_[rows, 0:nq]),
                             reads=[r_ps[obank + 1]], writes=[r_rl])
                        p.op("dve", lambda e: e.tensor_tensor(out=ybT[rows, q0:q0 + nq], in0=ops_[rows, 0:nq], in1=rl[rows, 0:nq], op=ALU.mult),
                             reads=[r_ps[obank], r_rl], writes=[r_ybt[min(q0 // 512, 4)]])
                        if gi in tile_done:
                            for k_, job in enumerate(tile_done[gi]):
                                pending_cb.append((si + 6 + k_, job))

                for k_ in range(len(pending_cb)):
                    pending_cb[k_] = (-1, pending_cb[k_][1])
                emit_qk(0)
                if len(steps) > 1:
                    emit_qk(1)
                for si in range(len(steps)):
                    if si + 2 < len(steps):
                        emit_qk(si + 2)
                    emit_rest(si)
                    if pending_cb and pending_cb[0][0] <= si:
                        pending_cb.pop(0)[1]()
                if flush:
                    while pending_cb:
                        pending_cb.pop(0)[1]()

            for pr in range(4):
                itA = wq.next()
                itO = wq.next()
                wA = itA.ap.rearrange("p (m c j) -> p m c j", m=2, c=DC)
                p.dma("pool", lambda e, pr=pr: e.dma_start(out=cKT[:, :], in_=ckt_in[pr * 128:(pr + 1) * 128, :]), k_ck,
                      writes=[r_ck] + (r_vst_x if pr == 0 else []))
                p.dma("pool", lambda e, pr=pr: e.dma_start(
                    out=cVb.rearrange("p j n -> p (j n)"), in_=cv_in[pr]), k_ck, writes=[r_ck])
                for t, (t0, n) in enumerate(TT):
                    b = t % 2
                    pq, pk = PS[b], PS[6 + b]
                    for c in range(DC):
                        mm(pq[:, :n], wA[:, 0, c, :], xn[:, c, t0:t0 + n], c == 0, c == DC - 1, [itA.res, r_xn[c][t]], [r_ps[b]])
                    for c in range(DC):
                        mm(pk[:, :n], wA[:, 1, c, :], xn[:, c, t0:t0 + n], c == 0, c == DC - 1, [itA.res, r_xn[c][t]], [r_ps[6 + b]])
                    p.op("act", lambda e, n=n, t0=t0, pq=pq: e.activation(out=Qz[0][0:64, t0:t0 + n], in_=pq[0:64, :n], func=AF.Copy),
                         reads=[r_ps[b]], writes=[r_qk])
                    p.op("act", lambda e, n=n, t0=t0, pq=pq: e.activation(out=Qz[1][64:128, t0:t0 + n], in_=pq[64:128, :n], func=AF.Copy),
                         reads=[r_ps[b]], writes=[r_qk])
                    p.op("act", lambda e, n=n, t0=t0, pk=pk: e.activation(out=KT[:, t0:t0 + n], in_=pk[:, :n], func=AF.Copy),
                         reads=[r_ps[6 + b]], writes=[r_qk])
                    p.op("dve", lambda e, n=n, b=b, pk=pk: e.tensor_copy(out=kst[b][:, :n], in_=pk[:, :n]),
                         reads=[r_ps[6 + b]], writes=[r_kst[b]])
                    p.dma("sp", lambda e, n=n, b=b, t0=t0, pr=pr: e.dma_start(
                        out=kT_o[pr * 128:(pr + 1) * 128, t0:t0 + n], in_=kst[b][:, :n]), r_kst[b], reads=[r_kst[b]])
                wq.release(itA)
                if pr == 0:
                    bidx = emit_cumsum()
                def prompt_group(hh, Q):
                    h = 2 * pr + hh
                    kts = []
                    for j in range(4 * Q + 4):
                        n0 = (j - 4 * Q) * 128 if j >= 4 * Q else None
                        bi = bidx[(Q, j)]
                        kts.append((KT[:, j * 128:(j + 1) * 128], Vb[:, j, pr * 128:(pr + 1) * 128],
                                    biasP[:, bi * 8 + h:bi * 8 + h + 1], 128, n0, []))
                    return (hh, Q * 512, 512, kts)

                def sample_group(hh):
                    h = 2 * pr + hh
                    kts = []
                    for j in range(16):
                        kts.append((cKT[:, j * 128:(j + 1) * 128], cVb[:, j, :], biasS[:, j * 8 + h:j * 8 + h + 1],
                                    128, None, [r_ck]))
                    kts.append((KT[:, SEQ:NT], Vb[0:16, 16, pr * 128:(pr + 1) * 128],
                                biasS[0:16, 128 + h:128 + h + 1], 16, 0, []))
                    return (hh, SEQ, NS, kts)

                groups = [sample_group(0), sample_group(1)]
                tiles_order = [4]
                for Q in range(4):
                    groups += [prompt_group(0, Q), prompt_group(1, Q)]
                    tiles_order.append(Q)
                last = pr == 3
                if last:
                    set_sq8(em_sq8)
                prev = [None]

                def proj_units(t, itO=itO, last=last):
                    t0, n = TT[t]
                    jobs = []
                    for c in range(DC):
                        def unit(c=c):
                            b = 0
                            mm(PS[b][:, :n], itO.ap[:, c * 128:(c + 1) * 128], ybT[:, t0:t0 + n], True, True, [itO.res, r_ybt[t]], [r_ps[b]])
                            p.op("dve", lambda e: e.tensor_tensor(out=xT[:, c, t0:t0 + n], in0=PS[b][:, :n], in1=xT[:, c, t0:t0 + n], op=ALU.add),
                                 reads=[r_ps[b], r_xT[c][t]], writes=[r_xT[c][t]])
                        jobs.append(unit)
                    if last:
                        def nrm(t=t):
                            if prev[0] is not None:
                                norm_tile(prev[0], nxt[0], ssb=0, final=nxt[1])
                            prev[0] = t
                        jobs.append(nrm)
                    if t == 3:
                        jobs.append(lambda itO=itO: wq.release(itO))
                    return jobs

                tile_done = {2 * i + 1: proj_units(t) for i, t in enumerate(tiles_order)}
                attn_run(groups, tile_done, flush=last)
                if last:
                    norm_tile(prev[0], nxt[0], ssb=0, final=nxt[1])
            p.barrier()

        def odd_mixer(nxt):
            cv = Carver()
            base_off = BASE
            cv.off = base_off

            vn = cv.bf16(17 * 512).rearrange("p (j n) -> p j n", j=17)
            r_vn = p.res("vn")
            uT = cv.bf16(4 * NT).rearrange("p (c n) -> p c n", c=4)
            r_uT = p.res("uT")
            gvf = [cv.f32(512) for _ in range(4)]
            r_gvf = [p.res(f"gvf{b}") for b in range(4)]
            stats = [cv.f32(8) for _ in range(4)]
            r_st = [p.res(f"st{b}") for b in range(4)]
            mv4 = cv.f32(8).rearrange("p (q k) -> p q k", k=2)
            r_mv4 = [p.res(f"mv{b}") for b in range(2)]
            bsr = cv.f32(1024)
            r_bsr = p.res("bsr")
            p.dma("sp", lambda e: e.dma_start(out=bsr[0:1, :], in_=bs_in), r_bsr, writes=[r_bsr])
            bs_hi = cv.bf16(1024)
            bs_lo = cv.bf16(1024)
            bs_t = cv.f32(1024)
            r_bs2 = p.res("bs2")
            p.op("dve", lambda e: e.tensor_copy(out=bs_hi[0:1, :], in_=bsr[0:1, :]), reads=[r_bsr], writes=[r_bs2])
            p.op("dve", lambda e: e.tensor_copy(out=bs_t[0:1, :], in_=bs_hi[0:1, :]), reads=[r_bs2], writes=[r_bs2])
            p.op("dve", lambda e: e.tensor_tensor(out=bs_t[0:1, :], in0=bsr[0:1, :], in1=bs_t[0:1, :], op=ALU.subtract),
                 reads=[r_bsr, r_bs2], writes=[r_bs2])
            p.op("dve", lambda e: e.tensor_copy(out=bs_lo[0:1, :], in_=bs_t[0:1, :]), reads=[r_bs2], writes=[r_bs2])
            wv = [wq.next() for _ in range(8)]
            itU = [wq.next() for _ in range(2)]
            ujobs = [(cc, t) for cc in range(4) for t in range(5)]
            ucnt = [0]

            def u_job():
                if ucnt[0] >= len(ujobs):
                    return
                cc, t = ujobs[ucnt[0]]
                t0, n = TT[t]
                b = 2 + ucnt[0] % 2
                ucnt[0] += 1
                wu_ = itU[cc // 2].ap.rearrange("p (m c j) -> p m c j", m=2, c=DC)[:, cc % 2]
                for c in range(DC):
                    mm(PS[b][:, :n], wu_[:, c, :], xn[:, c, t0:t0 + n], c == 0, c == DC - 1, [itU[cc // 2].res, r_xn[c][t]], [r_ps[b]])
                p.op("act", lambda e: e.activation(out=uT[:, cc, t0:t0 + n], in_=PS[b][:, :n], func=AF.Gelu_apprx_tanh),
                     reads=[r_ps[b]], writes=[r_uT])

            def vv_finish(js):
                q0 = js[0] % 4
                nq_ = len(js)
                ps_ = (js[0] // 2) % 2
                nmax = TK[js[0]][1]
                p.op("act", lambda e: e.activation(out=mv4[:nmax, q0:q0 + nq_, 1:2], in_=mv4[:nmax, q0:q0 + nq_, 1:2], func=AF.Sqrt, bias=EPS, scale=1.0),
                     reads=[r_mv4[ps_]], writes=[r_mv4[ps_]])
                p.op("dve", lambda e: e.reciprocal(out=mv4[:nmax, q0:q0 + nq_, 1:2], in_=mv4[:nmax, q0:q0 + nq_, 1:2]),
                     reads=[r_mv4[ps_]], writes=[r_mv4[ps_]])
                for j in js:
                    t0, n = TK[j]
                    q = j % 4
                    g_ = gvf[q]
                    p.op("dve", lambda e, n=n, q=q, g_=g_: e.tensor_scalar(
                        out=g_[:n, :], in0=g_[:n, :], scalar1=mv4[:n, q, 0:1], scalar2=mv4[:n, q, 1:2], op0=ALU.subtract, op1=ALU.mult),
                        reads=[r_mv4[ps_], r_gvf[q]], writes=[r_gvf[q]])
                    p.op("dve", lambda e, n=n, g_=g_: e.tensor_tensor(out=g_[:n, :], in0=g_[:n, :], in1=pbc[:n, PB_LNG:PB_LNG + 512], op=ALU.mult),
                         reads=[r_gvf[q], r_pv], writes=[r_gvf[q]])
                    if j == 16:
                        p.op("dve", lambda e, n=n, g_=g_: e.tensor_tensor(out=g_[:n, :], in0=g_[:n, :], in1=pbc[:n, PB_LNB:PB_LNB + 512], op=ALU.add),
                             reads=[r_gvf[q], r_pv], writes=[r_gvf[q]])
                        p.op("act", lambda e, n=n, g_=g_, j=j: e.activation(out=vn[:n, j, :], in_=g_[:n, :], func=AF.Copy),
                             reads=[r_gvf[q]], writes=[r_vn])
                        p.dma("sp", lambda e, g_=g_: e.dma_start(out=gv_o[:, :], in_=g_[0:NS, :]), r_gvf[q], reads=[r_gvf[q]])
                    else:
                        p.op("dve", lambda e, n=n, g_=g_, j=j: e.tensor_tensor(out=vn[:n, j, :], in0=g_[:n, :], in1=pbc[:n, PB_LNB:PB_LNB + 512], op=ALU.add),
                             reads=[r_gvf[q], r_pv], writes=[r_vn])

            for j, (t0, n) in enumerate(TK):
                u_job()
                b = j % 2
                q = j % 4
                ps_ = (j // 2) % 2
                vps = PS[b]
                for c in range(DC):
                    mm(vps[:n, :], xn[:, c, t0:t0 + n], wv[c].ap[:, 0:512], c == 0, c == DC - 1,
                       [wv[c].res, r_xn[c][min(j // 4, 4)]], [r_ps[b]])
                g_ = gvf[q]
                p.op("act", lambda e, n=n, g_=g_, vps=vps: e.activation(out=g_[:n, :], in_=vps[:n, :], func=AF.Gelu_apprx_tanh),
                     reads=[r_ps[b]], writes=[r_gvf[q]])
                p.op("dve", lambda e, n=n, g_=g_, q=q: e.bn_stats(out=stats[q][:n, 0:6], in_=g_[:n, :]),
                     reads=[r_gvf[q]], writes=[r_st[q]])
                p.op("dve", lambda e, n=n, q=q: e.bn_aggr(out=mv4[:n, q, :], in_=stats[q][:n, 0:6]),
                     reads=[r_st[q]], writes=[r_mv4[ps_]])
                if j % 2 == 1:
                    vv_finish([j - 1, j])
                elif j == 16:
                    vv_finish([16])
            while ucnt[0] < len(ujobs):
                u_job()
            for it in wv:
                wq.release(it)
            for it in itU:
                wq.release(it)
            mark(p, 'om_spatial')
            bsh3 = bs_hi[0:1, :].rearrange("p (g t) -> p g t", g=8)
            bsl3 = bs_lo[0:1, :].rearrange("p (g t) -> p g t", g=8)
            for pr in range(4):
                for t, (t0, n) in enumerate(TT):
                    banks = [(PS[4 + (t % 2) * 2], r_ps[4 + (t % 2) * 2]), (PS[5 + (t % 2) * 2], r_ps[5 + (t % 2) * 2])]
                    nch = 4 if t < 4 else 1
                    L = 128 if t < 4 else 16
                    for ch in range(nch):
                        j = t * 4 + ch
                        pp, rr = banks[ch // 2]
                        c0 = (ch % 2) * 2 * L
                        o3 = pp[:, c0:c0 + 2 * L].rearrange("p (g t) -> p g t", g=2)
                        mm(o3, vn[:L, j, pr * 128:(pr + 1) * 128], wstm[:L, 2 * pr:2 * pr + 2, 0:L], True, False, [r_vn, r_pv], [rr])
                        mm(o3, ones_b[0:1, :], bsh3[:, 2 * pr:2 * pr + 2, 0:L], False, False, [r_bs2, r_const], [rr])
                        mm(o3, ones_b[0:1, :], bsl3[:, 2 * pr:2 * pr + 2, 0:L], False, True, [r_bs2, r_const], [rr])
                    for bi_, (pp, rr) in enumerate(banks):
                        nb_ = min(2, nch - 2 * bi_)
                        if nb_ <= 0:
                            continue
                        cb = t0 + 2 * bi_ * L
                        for gi_, rows in enumerate((slice(0, 64), slice(64, 128))):
                            src = pp[rows, 0:nb_ * 2 * L].rearrange("p (ch g t) -> p ch g t", g=2, t=L)[:, :, gi_, :]
                            dst = uT[rows, pr, cb:cb + nb_ * L].rearrange("p (ch t) -> p ch t", t=L)
                            p.op("dve", lambda e, src=src, dst=dst: e.tensor_tensor(out=dst, in0=src, in1=dst, op=ALU.mult),
                                 reads=[rr, r_uT], writes=[r_uT])
            wo = [wq.next() for _ in range(4)]
            project(wo, [uT[:, k, :] for k in range(4)], [r_uT], bank0=6)
            for it in wo:
                wq.release(it)
            p.barrier()

            mark(p, 'om_D1')
            cv.off = base_off
            WP = 30 + SEQ
            WS = 30 + NS
            off_gbf = cv.off
            gbf = cv.bf16(4 * (WP + WS)).rearrange("p (c n) -> p c n", c=4)
            r_g = [p.res(f"g{c}") for c in range(4)]
            gtail = cv.f32(4 * 30).rearrange("p (c n) -> p c n", c=4)
            sctx = cv.f32(4 * 30).rearrange("p (c n) -> p c n", c=4)
            gs32 = cv.f32(4 * 16).rearrange("p (c n) -> p c n", c=4)
            r_gt = [p.res(f"gt{c}") for c in range(4)]
            sgb = [cv.f32(512) for _ in range(2)]
            r_sgb = [p.res(f"sgb{b}") for b in range(2)]
            off_d1 = cv.off
            dg = [cv.bf16(31 * 128).rearrange("p (j n) -> p j n", j=31) for _ in range(2)]
            r_dg = [p.res(f"dg{b}") for b in range(2)]
            off_d2 = cv.off

            def dg_op(cc, j):
                db = cc % 2
                p.op("act", lambda e: e.activation(
                    out=dg[db][:, j, :], in_=ident_f[:], func=AF.Copy, scale=pv[:, PV_DW + j * 4 + cc:PV_DW + j * 4 + cc + 1]),
                    reads=[r_const, r_pv], writes=[r_dg[db]])

            def build_dg(cc):
                for j in range(31):
                    dg_op(cc, j)

            dg_pending = [(cc, j) for cc in (0, 1) for j in range(31)]
            itG = [wq.next() for _ in range(4)]
            cnt = 0
            for cc in range(4):
                w2 = itG[cc].ap.rearrange("p (m c j) -> p m c j", m=2, c=DC)
                p.op("dve", lambda e, cc=cc: e.memset(gbf[:, cc, 0:30], 0.0), writes=[r_g[cc]])
                p.dma("sp", lambda e, cc=cc: e.dma_start(out=sctx[:, cc, :], in_=scc_in[cc * 128:(cc + 1) * 128, :]),
                      r_gt[cc], writes=[r_gt[cc]])
                p.op("dve", lambda e, cc=cc: e.tensor_copy(out=gbf[:, cc, WP:WP + 30], in_=sctx[:, cc, :]),
                     reads=[r_gt[cc]], writes=[r_g[cc]])
                for t, (t0, n) in enumerate(TT):
                    b = cnt % 2
                    cnt += 1
                    pga, pgb = PS[b], PS[2 + b]
                    for c in range(DC):
                        mm(pga[:, :n], w2[:, 0, c, :], xn[:, c, t0:t0 + n], c == 0, c == DC - 1, [itG[cc].res, r_xn[c][t]], [r_ps[b]])
                    for c in range(DC):
                        mm(pgb[:, :n], w2[:, 1, c, :], xn[:, c, t0:t0 + n], c == 0, c == DC - 1, [itG[cc].res, r_xn[c][t]], [r_ps[2 + b]])
                    g0 = 30 + t0 if t < 4 else WP + 30
                    p.op("act", lambda e, b=b, n=n, pgb=pgb: e.activation(out=sgb[b][:, :n], in_=pgb[:, :n], func=AF.Sigmoid),
                         reads=[r_ps[2 + b]], writes=[r_sgb[b]])
                    for _ in range(4):
                        if dg_pending:
                            dg_op(*dg_pending.pop(0))
                    p.op("dve", lambda e, b=b, n=n, cc=cc, g0=g0, pga=pga: e.tensor_tensor(
                        out=gbf[:, cc, g0:g0 + n], in0=pga[:, :n], in1=sgb[b][:, :n], op=ALU.mult),
                        reads=[r_ps[b], r_sgb[b]], writes=[r_g[cc]])
                    if t == 3:
                        p.op("dve", lambda e, b=b, cc=cc, pga=pga: e.tensor_tensor(
                            out=gtail[:, cc, :], in0=pga[:, 482:512], in1=sgb[b][:, 482:512], op=ALU.mult),
                            reads=[r_ps[b], r_sgb[b]], writes=[r_gt[cc]])
                    if t == 4:
                        p.op("dve", lambda e, b=b, cc=cc, pga=pga: e.tensor_tensor(
                            out=gs32[:, cc, :], in0=pga[:, 0:16], in1=sgb[b][:, 0:16], op=ALU.mult),
                            reads=[r_ps[b], r_sgb[b]], writes=[r_gt[cc]])
                wq.release(itG[cc])
                rs = slice(cc * 128, (cc + 1) * 128)
                p.dma("sp", lambda e, cc=cc, rs=rs: e.dma_start(out=ccp_o[rs, :], in_=gtail[:, cc, :]), r_gt[cc], reads=[r_gt[cc]])
                p.dma("sp", lambda e, cc=cc, rs=rs: e.dma_start(out=ccs_o[rs, 0:14], in_=sctx[:, cc, 16:30]), r_gt[cc], reads=[r_gt[cc]])
                p.dma("sp", lambda e, cc=cc, rs=rs: e.dma_start(out=ccs_o[rs, 14:30], in_=gs32[:, cc, :]), r_gt[cc], reads=[r_gt[cc]])
            while dg_pending:
                dg_op(*dg_pending.pop(0))
            p.barrier()
            mark(p, 'om_D2conv')
            cv.off = 0
            cdf = cv.f32(4 * NT).rearrange("p (c n) -> p c n", c=4)
            r_cd = [p.res(f"cd{t}") for t in range(5)]
            assert cv.off <= XN_W, (cv.off, XN_W)
            cnt = 0
            for cc in range(4):
                db = cc % 2
                dg_next = [(cc + 1, j) for j in range(31)] if 1 <= cc < 3 else []
                for t, (t0, n) in enumerate(TT):
                    b = cnt % 2
                    cnt += 1
                    g0 = 30 + t0 if t < 4 else WP + 30
                    for j in range(31):
                        mm(PS[b][:, :n], dg[db][:, j, :], gbf[:, cc, g0 - 30 + j:g0 - 30 + j + n], j == 0, j == 30,
                           [r_dg[db], r_g[cc]], [r_ps[b]])
                    p.op("act", lambda e, b=b, n=n, cc=cc, t0=t0: e.activation(
                        out=cdf[:, cc, t0:t0 + n], in_=PS[b][:, :n], func=AF.Identity, bias=pv[:, PV_DWB + cc:PV_DWB + cc + 1], scale=1.0),
                        reads=[r_ps[b], r_pv], writes=[r_cd[t]])
                    for _ in range(8):
                        if dg_next:
                            dg_op(*dg_next.pop(0))
                while dg_next:
                    dg_op(*dg_next.pop(0))
            p.barrier()
            mark(p, 'om_D3ln')
            cv.off = off_gbf
            ydT = cv.bf16(4 * NT).rearrange("p (c n) -> p c n", c=4)
            r_yd = p.res("yd")
            assert cv.off <= off_d1
            cv.off = off_d1
            sqd = [cv.bf16(512) for _ in range(2)]
            r_sqd = [p.res(f"sqd{b}") for b in range(2)]
            mu2 = [cv.f32(512) for _ in range(2)]
            ex22 = [cv.f32(512) for _ in range(2)]
            r_mu2 = [p.res(f"mu{b}") for b in range(2)]
            tmp = [cv.f32(512) for _ in range(2)]
            r_tmp = [p.res(f"tmp{b}") for b in range(2)]
            set_sq8(cv.bf16(DC * 512))

            def ln_stats(t):
                t0, n = TT[t]
                b = t % 2
                s1, s2 = PS[b], PS[2 + b]
                mu, ex2, r_mu = mu2[b], ex22[b], r_mu2[b]
                for cc in range(4):
                    mm(s1[:, :n], ones_f[:], cdf[:, cc, t0:t0 + n], cc == 0, cc == 3, [r_cd[t], r_const], [r_ps[b]])
                for cc in range(4):
                    sb2 = cc % 2
                    p.op("act", lambda e, cc=cc, sb2=sb2: e.activation(out=sqd[sb2][:, :n], in_=cdf[:, cc, t0:t0 + n], func=AF.Square),
                         reads=[r_cd[t]], writes=[r_sqd[sb2]])
                    mm(s2[:, :n], ones_b[:], sqd[sb2][:, :n], cc == 0, cc == 3, [r_sqd[sb2], r_const], [r_ps[2 + b]])
                p.op("act", lambda e: e.activation(out=mu[:, :n], in_=s1[:, :n], func=AF.Copy, scale=1.0 / 512),
                     reads=[r_ps[b]], writes=[r_mu])
                p.op("dve", lambda e: e.tensor_tensor(out=ex2[:, :n], in0=mu[:, :n], in1=mu[:, :n], op=ALU.mult),
                     reads=[r_mu], writes=[r_mu])
                p.op("dve", lambda e: e.scalar_tensor_tensor(out=ex2[:, :n], in0=s2[:, :n], scalar=1.0 / 512, in1=ex2[:, :n],
                                                             op0=ALU.mult, op1=ALU.subtract),
                     reads=[r_ps[2 + b], r_mu], writes=[r_mu])
                p.op("act", lambda e: e.activation(out=ex2[:, :n], in_=ex2[:, :n], func=AF.Ln, bias=EPS, scale=1.0),
                     reads=[r_mu], writes=[r_mu])
                p.op("act", lambda e: e.activation(out=ex2[:, :n], in_=ex2[:, :n], func=AF.Exp, scale=-0.5),
                     reads=[r_mu], writes=[r_mu])

            def ln_apply(t):
                t0, n = TT[t]
                b = t % 2
                mu, ex2, r_mu = mu2[b], ex22[b], r_mu2[b]
                for cc in range(4):
                    tb = cc % 2
                    p.op("dve", lambda e, cc=cc, tb=tb: e.tensor_tensor(out=tmp[tb][:, :n], in0=cdf[:, cc, t0:t0 + n], in1=mu[:, :n], op=ALU.subtract),
                         reads=[r_cd[t], r_mu], writes=[r_tmp[tb]])
                    p.op("dve", lambda e, tb=tb: e.tensor_tensor(out=tmp[tb][:, :n], in0=tmp[tb][:, :n], in1=ex2[:, :n], op=ALU.mult),
                         reads=[r_tmp[tb], r_mu], writes=[r_tmp[tb]])
                    p.op("act", lambda e, cc=cc, tb=tb: e.activation(
                        out=ydT[:, cc, t0:t0 + n], in_=tmp[tb][:, :n], func=AF.Silu,
                        scale=pv[:, PV_CNG + cc:PV_CNG + cc + 1], bias=pv[:, PV_CNB + cc:PV_CNB + cc + 1]),
                        reads=[r_tmp[tb], r_pv], writes=[r_yd])

            ln_stats(0)
            for t in range(5):
                if t + 1 < 5:
                    ln_stats(t + 1)
                ln_apply(t)
            wo = [wq.next() for _ in range(4)]
            project(wo, [ydT[:, k, :] for k in range(4)], [r_yd], bank0=4, nxt=nxt)
            for it in wo:
                wq.release(it)
            p.barrier()

        add_ffn_w(0)
        add_even_w()
        add_ffn_w(1)
        add_ffn_w(2)
        add_odd_w()
        add_ffn_w(3)
        set_sq8(arena[:, BASE + 10320 + 512:BASE + 10320 + 512 + DC * 256].bitcast(BF16))
        for t in range(5):
            norm_tile(t, 0)
        mark(p, 'ffn0')
        ffn(0, (1, False))
        mark(p, 'even')
        even_mixer((2, False))
        mark(p, 'ffn1')
        ffn(1, (3, False))
        mark(p, 'ffn2')
        ffn(2, (4, False))
        mark(p, 'odd')
        odd_mixer((5, False))
        mark(p, 'ffn3')
        ffn(3, (6, True))
        mark(p, 'end')
        p.emit(nc, st)
    return nc


_NC_CACHE = {}
MARKS = []


def mark(p, name):
    MARKS.append((name, len(p.streams['pe'])))


def _prep(x_prompt, x_sample, cache_fox_k, cache_fox_v, cache_fox_logf, state_sconv, state_cconv,
           ffn1_g, ffn1_wg, ffn1_wu, ffn1_wd, mix_g, ffn2_g, ffn2_wg, ffn2_wu, ffn2_wd,
           e_w_in, e_b_f, e_conv_w, e_w_out, o_w_in, o_ln_g, o_ln_b, o_ws, o_bs,
           o_dw, o_dw_b, o_cn_g, o_cn_b, o_w_out, final_g):
    f = lambda a: np.ascontiguousarray(np.asarray(a, dtype=np.float32))
    x_prompt, x_sample = f(x_prompt), f(x_sample)
    wg = [f(ffn1_wg)[0], f(ffn2_wg)[0], f(ffn1_wg)[1], f(ffn2_wg)[1]]
    wu = [f(ffn1_wu)[0], f(ffn2_wu)[0], f(ffn1_wu)[1], f(ffn2_wu)[1]]
    wd = np.stack([f(ffn1_wd)[0], f(ffn2_wd)[0], f(ffn1_wd)[1], f(ffn2_wd)[1]])
    wgu = np.empty((4, NF, 128, 2, 8, 128), np.float32)
    for i in range(4):
        wgu[i, :, :, 0] = wg[i].reshape(8, 128, NF, 128).transpose(2, 1, 0, 3)
        wgu[i, :, :, 1] = wu[i].reshape(8, 128, NF, 128).transpose(2, 1, 0, 3)
    wgu = wgu.reshape(4, NF, 128, 2048)

    def chunks(w, order):
        n = len(order)
        o = np.empty((n // 2, 128, 2, 8, 128), np.float32)
        for k, ch in enumerate(order):
            o[k // 2, :, k % 2] = w[:, ch * 128:(ch + 1) * 128].reshape(8, 128, 128).transpose(1, 0, 2)
        return o.reshape(n // 2, 128, 2048)

    ew = f(e_w_in)[0]
    ewa = chunks(ew, E_ORDER)
    ewv = np.ascontiguousarray(ew[:, 2560:3080].reshape(8, 128, 520).transpose(1, 0, 2))
    ow = f(o_w_in)[0]
    owa = chunks(ow, O_ORDER)
    owv = np.ascontiguousarray(ow[:, 512:1024].reshape(8, 128, 512).transpose(1, 0, 2))
    pvh = np.zeros((128, NPV), np.float32)
    gains = [f(ffn1_g)[0], f(mix_g)[0], f(ffn2_g)[0], f(ffn1_g)[1], f(mix_g)[1], f(ffn2_g)[1], f(final_g)]
    for k, g in enumerate(gains):
        pvh[:, k * 8:(k + 1) * 8] = g.reshape(8, 128).T
    pvh[:, PV_CW:PV_CW + 12] = f(e_conv_w)[0].reshape(3, 4, 128).transpose(2, 0, 1).reshape(128, 12)
    pvh[:, PV_DW:PV_DW + 124] = f(o_dw)[0].reshape(31, 4, 128).transpose(2, 0, 1).reshape(128, 124)
    pvh[:, PV_DWB:PV_DWB + 4] = f(o_dw_b)[0].reshape(4, 128).T
    pvh[:, PV_CNG:PV_CNG + 4] = f(o_cn_g)[0].reshape(4, 128).T
    pvh[:, PV_CNB:PV_CNB + 4] = f(o_cn_b)[0].reshape(4, 128).T
    pbh = np.zeros((128, NPB), np.float32)
    pbh[:, PB_LNG:PB_LNG + 512] = f(o_ln_g)[0][None, :]
    pbh[:, PB_LNB:PB_LNB + 512] = f(o_ln_b)[0][None, :]
    pbh[:, PB_BF:PB_BF + 136] = np.tile(f(e_b_f)[0], 17)[None, :]
    wst = np.ascontiguousarray(f(o_ws)[0].transpose(2, 0, 1))
    bsr = f(o_bs)[0].reshape(1, 1024)
    ck, cvv, clf = f(cache_fox_k)[0], f(cache_fox_v)[0], f(cache_fox_logf)[0]
    ssc, scc = f(state_sconv)[0], f(state_cconv)[0]
    shared = dict(wgu=wgu, wd=wd, ewa=ewa, ewv=ewv, ewo=f(e_w_out)[0], owa=owa, owv=owv, owo=f(o_w_out)[0],
                  pv=pvh, pb=pbh, wst=wst, bsr=bsr)
    in_maps = []
    for b in range(NCORES):
        m = dict(shared)
        m["xT_in"] = np.ascontiguousarray(np.concatenate([x_prompt[b], x_sample[b]], axis=0).T)
        m["ckt"] = np.ascontiguousarray(ck[b].reshape(SEQ, 512).T)
        m["cv"] = np.ascontiguousarray(cvv[b].reshape(16, 128, 4, 128).transpose(2, 1, 0, 3)).reshape(4, 128, 2048)
        m["clf"] = np.ascontiguousarray(clf[b].reshape(16, 128, 8).transpose(1, 0, 2)).reshape(128, 128)
        m["ssc"] = np.ascontiguousarray(ssc[b].T)
        m["scc"] = np.ascontiguousarray(scc[b].T)
        in_maps.append(m)
    return in_maps


def _post(R):
    NCORES = len(R)
    st_ = lambda name, fn: np.stack([fn(np.asarray(R[b][name])) for b in range(NCORES)])
    y_p = st_("yT", lambda a: a[:, :SEQ].T)
    y_s = st_("yT", lambda a: a[:, SEQ:].T)
    pk = st_("kT", lambda a: a[:, :SEQ].T.reshape(SEQ, 8, 64))[None]
    sk = st_("kT", lambda a: a[:, SEQ:].T.reshape(NS, 8, 64))[None]
    pvv = st_("vv", lambda a: a[:SEQ].reshape(SEQ, 8, 64))[None]
    svv = st_("vv", lambda a: a[SEQ:].reshape(NS, 8, 64))[None]
    plf = st_("lf", lambda a: a[:, :128].reshape(128, 16, 8).transpose(1, 0, 2).reshape(SEQ, 8))[None]
    slf = st_("lf", lambda a: a[:16, 128:136])[None]
    psc = st_("scp", lambda a: a.T)[None]
    ssc_o = st_("scs", lambda a: a.T)[None]
    pcc = st_("ccp", lambda a: a.T)[None]
    scc_o = st_("ccs", lambda a: a.T)[None]
    gv = st_("gv", lambda a: a)[None]
    outs = (y_p, y_s, pk, pvv, plf, psc, pcc, sk, svv, slf, ssc_o, scc_o, gv)
    return tuple(np.ascontiguousarray(o, dtype=np.float32) for o in outs)


def kernel(**inputs):
    in_maps = _prep(**inputs)
    if "nc" not in _NC_CACHE:
        _NC_CACHE["nc"] = build_nc()
    res = run_bass_kernel_spmd(_NC_CACHE["nc"], in_maps, core_ids=list(range(NCORES)))
    return _post(res.results)
```
